# Optimizing a Trainium2 kernel written in Bass

```python
import math
import jax, jax.numpy as jnp
from jax import lax
import numpy as np

D_MODEL = 1024
BATCH = 8
SEQ = 2048
DEPTH = 4

CHUNK = 64
N_HEADS = 8
N_KV_HEADS = 2
HEAD_DIM = 64
ATTN_WIDTH = N_HEADS * HEAD_DIM
ROPE_DIM = HEAD_DIM // 4
ROPE_THETA = 500000.0
N_IDX_HEADS = 4
IDX_DIM = 64
TOPK_MAX = 256
Q_BLOCK = 128
SGU_CHUNK = 128
SGU_GROUPS = 8
SGU_WIDTH = D_MODEL - ATTN_WIDTH
SGU_GROUP_DIM = SGU_WIDTH // SGU_GROUPS
D_FF = -(-(8 * D_MODEL) // (3 * 256)) * 256
PLE_DIM = 256
ALPHA = (2 * DEPTH) ** 0.25
BETA = (8 * DEPTH) ** -0.25
LN_EPS = 1e-5

Q_COLS = N_HEADS * HEAD_DIM
KV_COLS = N_KV_HEADS * HEAD_DIM
IQ_COLS = N_IDX_HEADS * IDX_DIM
IK_COLS = IDX_DIM
IW_COLS = N_IDX_HEADS
IN_SPLITS = (Q_COLS, KV_COLS, KV_COLS, IQ_COLS, IK_COLS, IW_COLS, SGU_WIDTH, SGU_WIDTH)
IN_COLS = Q_COLS + 2 * KV_COLS + IQ_COLS + IK_COLS + IW_COLS + 2 * SGU_WIDTH

kernel_name = "hybrid_dsa_gmlp_deepnorm_encoder"


def layer_norm(x, g, b):
    xf = x.astype(jnp.float32)
    mu = jnp.mean(xf, axis=-1, keepdims=True)
    var = jnp.mean(jnp.square(xf - mu), axis=-1, keepdims=True)
    y = (xf - mu) * lax.rsqrt(var + LN_EPS)
    return (y * g.astype(jnp.float32) + b.astype(jnp.float32)).astype(x.dtype)


def partial_rope(x, pos):
    half = ROPE_DIM // 2
    inv = ROPE_THETA ** (-2.0 * jnp.arange(half, dtype=jnp.float32) / ROPE_DIM)
    ang = pos.astype(jnp.float32)[..., None] * inv
    cos = jnp.cos(ang)[:, :, None, :]
    sin = jnp.sin(ang)[:, :, None, :]
    xr = x[..., :ROPE_DIM].astype(jnp.float32)
    x1, x2 = xr[..., :half], xr[..., half:]
    rot = jnp.concatenate([x1 * cos - x2 * sin, x2 * cos + x1 * sin], axis=-1)
    return jnp.concatenate([rot.astype(x.dtype), x[..., ROPE_DIM:]], axis=-1)


def dsa_attention(q, k, v, qi, ki, wi):
    B, S = q.shape[0], q.shape[1]
    topk = min(TOPK_MAX, S // 4)
    nb = S // Q_BLOCK
    G = N_KV_HEADS
    R = N_HEADS // N_KV_HEADS
    key_chunk = jnp.arange(S) // CHUNK

    def to_blocks(a):
        return jnp.moveaxis(a.reshape((B, nb, Q_BLOCK) + a.shape[2:]), 1, 0)

    q_b = to_blocks(q.reshape(B, S, G, R, HEAD_DIM))
    qi_b = to_blocks(qi)
    wi_b = to_blocks(wi)
    t_b = jnp.arange(S).reshape(nb, Q_BLOCK)

    def block(args):
        qb, qib, wib, tb = args
        q_chunk = tb // CHUNK
        allowed = key_chunk[None, :] <= q_chunk[:, None]
        logits = jnp.einsum('bqhd,bsd->bqhs', qib, ki)
        score = jnp.einsum('bqhs,bqh->bqs', jax.nn.relu(logits), wib).astype(jnp.float32)
        score = jnp.where(allowed[None], score, -jnp.inf)
        _, idx = lax.top_k(score, topk)
        k_sel = jax.vmap(lambda kb, ib: kb[ib])(k, idx)
        v_sel = jax.vmap(lambda vb, ib: vb[ib])(v, idx)
        valid = key_chunk[idx] <= q_chunk[None, :, None]
        s = jnp.einsum('bqgrd,bqkgd->bqgrk', qb, k_sel).astype(jnp.float32) * (HEAD_DIM ** -0.5)
        s = jnp.where(valid[:, :, None, None, :], s, -jnp.inf)
        pr = jax.nn.softmax(s, axis=-1).astype(v.dtype)
        return jnp.einsum('bqgrk,bqkgd->bqgrd', pr, v_sel)

    out = lax.map(block, (q_b, qi_b, wi_b, t_b))
    return jnp.moveaxis(out, 0, 1).reshape(B, S, ATTN_WIDTH)


def spatial_gating(u, v, w_s, b_s, g, bta):
    B, S = u.shape[0], u.shape[1]
    v = layer_norm(v, g, bta)
    nc = S // SGU_CHUNK
    vb = v.reshape(B, nc, SGU_CHUNK, SGU_GROUPS, SGU_GROUP_DIM)
    tri = jnp.tril(jnp.ones((SGU_CHUNK, SGU_CHUNK), dtype=bool))
    w = jnp.where(tri[None], w_s, jnp.zeros_like(w_s))
    mixed = jnp.einsum('gts,bcsgd->bctgd', w, vb) + jnp.swapaxes(b_s, 0, 1)[None, None, :, :, None]
    return u * mixed.reshape(B, S, SGU_WIDTH)


def setup_inputs(seed: int = 0) -> dict:
    key = jax.random.key(seed)
    ks = jax.random.split(key, 20)

    def nrm(k, shape, scale):
        return jax.random.normal(k, shape, jnp.float32) * scale

    x = nrm(ks[0], (BATCH, SEQ, D_MODEL), 1.0)
    p = nrm(ks[1], (DEPTH, BATCH, SEQ, PLE_DIM), 1.0)
    start = jax.random.randint(ks[2], (BATCH, 1), 0, 64, dtype=jnp.int32) * CHUNK
    positions = (start + jnp.arange(SEQ, dtype=jnp.int32)[None, :]).astype(jnp.int32)
    return {
        "x": x,
        "p": p,
        "positions": positions,
        "w_in": nrm(ks[3], (DEPTH, D_MODEL, IN_COLS), D_MODEL ** -0.5),
        "w_o": nrm(ks[4], (DEPTH, D_MODEL, D_MODEL), BETA * D_MODEL ** -0.5),
        "ln1_g": 1.0 + nrm(ks[5], (DEPTH, D_MODEL), 0.02),
        "ln1_b": nrm(ks[6], (DEPTH, D_MODEL), 0.02),
        "ln2_g": 1.0 + nrm(ks[7], (DEPTH, D_MODEL), 0.02),
        "ln2_b": nrm(ks[8], (DEPTH, D_MODEL), 0.02),
        "sgu_w": nrm(ks[9], (DEPTH, SGU_GROUPS, SGU_CHUNK, SGU_CHUNK), SGU_CHUNK ** -0.5),
        "sgu_b": 1.0 + nrm(ks[10], (DEPTH, SGU_GROUPS, SGU_CHUNK), 0.02),
        "sgu_ln_g": 1.0 + nrm(ks[11], (DEPTH, SGU_WIDTH), 0.02),
        "sgu_ln_b": nrm(ks[12], (DEPTH, SGU_WIDTH), 0.02),
        "w_ffn_in": nrm(ks[13], (DEPTH, D_MODEL, 2 * D_FF), D_MODEL ** -0.5),
        "w_ffn_out": nrm(ks[14], (DEPTH, D_FF, D_MODEL), BETA * D_FF ** -0.5),
        "w_ple": nrm(ks[15], (DEPTH, PLE_DIM, D_MODEL), PLE_DIM ** -0.5),
        "w_ple_gate": nrm(ks[16], (DEPTH, D_MODEL, D_MODEL), D_MODEL ** -0.5),
    }


def reference(x, p, positions, w_in, w_o, ln1_g, ln1_b, ln2_g, ln2_b, sgu_w, sgu_b,
              sgu_ln_g, sgu_ln_b, w_ffn_in, w_ffn_out, w_ple, w_ple_gate):
    B, S = x.shape[0], x.shape[1]
    split_points = [int(o) for o in np.cumsum(IN_SPLITS)[:-1]]
    for i in range(DEPTH):
        h = x @ w_in[i]
        q, k, v, qi, ki, wi, u, vs = jnp.split(h, split_points, axis=-1)
        q = partial_rope(q.reshape(B, S, N_HEADS, HEAD_DIM), positions)
        k = partial_rope(k.reshape(B, S, N_KV_HEADS, HEAD_DIM), positions)
        v = v.reshape(B, S, N_KV_HEADS, HEAD_DIM)
        qi = partial_rope(qi.reshape(B, S, N_IDX_HEADS, IDX_DIM), positions)
        ki = partial_rope(ki.reshape(B, S, 1, IDX_DIM), positions)[:, :, 0]
        wi = wi * ((N_IDX_HEADS * IDX_DIM) ** -0.5)
        a_out = dsa_attention(q, k, v, qi, ki, wi)
        b_out = spatial_gating(jax.nn.gelu(u), jax.nn.gelu(vs), sgu_w[i], sgu_b[i],
                               sgu_ln_g[i], sgu_ln_b[i])
        mix = jnp.concatenate([a_out, b_out], axis=-1) @ w_o[i]
        x = layer_norm(ALPHA * x + mix, ln1_g[i], ln1_b[i])
        gate, up = jnp.split(x @ w_ffn_in[i], 2, axis=-1)
        ffn = (jax.nn.silu(gate) * up) @ w_ffn_out[i]
        ple = (p[i] @ w_ple[i]) * jax.nn.sigmoid(x @ w_ple_gate[i])
        x = layer_norm(ALPHA * x + ffn + ple, ln2_g[i], ln2_b[i])
    return x
```

```python
import math
import numpy as np
import concourse.bass as bass
import concourse.mybir as mybir
from concourse.bass_utils import run_bass_kernel_spmd

F32 = mybir.dt.float32
BF16 = mybir.dt.bfloat16
I32 = mybir.dt.int32
AF = mybir.ActivationFunctionType
ALU = mybir.AluOpType
AX = mybir.AxisListType

S = 2048
D = 1024
NT = 16
NG = 4
DFF = 2816
NFC = 22
INC = 2116
ALPHA = 8 ** 0.25
EPS = 1e-5
TOPK = 256
NBIS = 22
BIG = 1.0e30
SAME_ENG_SYNC = True
RING = 5
import os
KDBG = os.environ.get('KDBG', 'bc')

CH = [(0, 484), (484, 480), (964, 128), (1092, 512), (1604, 512)]


def _slot_cols():
    q_order = [0, 4, 1, 5, 2, 6, 3, 7]
    bases = [h * 64 for h in q_order] + [512, 576] + [768 + 64 * h for h in range(4)] + [1024]
    return bases


def w_in_perm():
    bases = _slot_cols()
    cols = []
    for b in bases:
        cols += [b + d for d in range(8)]
    for b in bases:
        cols += [b + 8 + d for d in range(8)]
    cols += [1088 + h for h in range(4)]
    for s in range(10, 15):
        cols += [bases[s] + d for d in range(16, 64)]
    for s in range(0, 10):
        cols += [bases[s] + d for d in range(16, 64)]
    cols += list(range(640, 768))
    cols += list(range(1092, 2116))
    assert len(cols) == INC and len(set(cols)) == INC
    return np.array(cols)


class Buf:
    __slots__ = ("name", "lastw", "readers", "tok")

    def __init__(self, name, tok=None):
        self.name = name
        self.lastw = None
        self.readers = []
        self.tok = tok


class DmaSem:
    def __init__(self, sem):
        self.sem = sem
        self.count = 0


class Prog:
    ENGS = ["pe", "act", "dve", "pool", "sp"]

    def __init__(self, nc, es):
        self.nc = nc
        self.es = es
        self.ops = []
        self.sems = {e: es.enter_context(nc.semaphore("sem_" + e)) for e in self.ENGS}
        self.ndma = 0

    def dmasem(self):
        self.ndma += 1
        return DmaSem(self.es.enter_context(self.nc.semaphore("dsem%d" % self.ndma)))

    def op(self, eng, fn, reads=(), writes=(), dma=None):
        idx = len(self.ops)
        deps = set()
        toks = set(b.tok for b in list(reads) + list(writes) if b.tok is not None)
        if toks:
            reads = list(reads) + [t for t in toks if t not in writes]
        for b in reads:
            if b.lastw is not None:
                deps.add(b.lastw)
        for b in writes:
            if b.lastw is not None:
                deps.add(b.lastw)
            deps.update(b.readers)
        for b in reads:
            b.readers.append(idx)
        for b in writes:
            b.lastw = idx
            b.readers = []
        deps.discard(idx)
        self.ops.append([eng, fn, deps, dma, False, None])
        return idx

    def emit(self):
        nc = self.nc
        ops = self.ops
        for o in ops:
            eng, fn, deps, dma, _, _ = o
            for d in deps:
                p = ops[d]
                if p[3] is None and dma is None and p[0] == eng and (eng == "pe" or not SAME_ENG_SYNC):
                    continue
                p[4] = True
        cnt = {e: 0 for e in self.ENGS}
        for o in ops:
            if o[3] is not None:
                o[3].count += 16
                o[5] = (o[3].sem, o[3].count)
            elif o[4]:
                cnt[o[0]] += 1
                o[5] = (self.sems[o[0]], cnt[o[0]])
        per = {e: [] for e in self.ENGS}
        for o in ops:
            per[o[0]].append(o)
        names = {"pe": "tensor", "act": "scalar", "dve": "vector", "pool": "gpsimd", "sp": "sync"}
        with nc.Block() as block:
            for e in self.ENGS:
                def body(h, e=e):
                    waited = {}
                    for o in per[e]:
                        for d in sorted(o[2]):
                            p = ops[d]
                            if p[5] is None:
                                continue
                            if p[3] is None and o[3] is None and p[0] == e and (e == "pe" or not SAME_ENG_SYNC):
                                continue
                            sem, val = p[5]
                            key = id(sem)
                            if waited.get(key, 0) >= val:
                                continue
                            h.wait_ge(sem, val)
                            waited[key] = val
                        ins = o[1](h)
                        if o[5] is not None and ins is not None:
                            ins.then_inc(o[5][0], 16 if o[3] is not None else 1)
                getattr(block, names[e])(body)


class _Stop(Exception):
    pass


def build(n_layers, dbg=False, stage_limit=10 ** 9):
    import contextlib

    def stage(n):
        if n > stage_limit:
            raise _Stop()
    nc = bass.Bass("TRN2", target_bir_lowering=False)
    L = n_layers
    x_d = nc.dram_tensor("x", [S, D], F32, kind="ExternalInput").ap()
    p_d = nc.dram_tensor("p", [L, S, 256], F32, kind="ExternalInput").ap()
    pos_d = nc.dram_tensor("pos", [128, NT], I32, kind="ExternalInput").ap()
    w_in_d = nc.dram_tensor("w_in", [L, D, INC], F32, kind="ExternalInput").ap()
    w_o_d = nc.dram_tensor("w_o", [L, D, D], F32, kind="ExternalInput").ap()
    vec_d = {}
    for nm, n in [("ln1_g", D), ("ln1_b", D), ("ln2_g", D), ("ln2_b", D), ("sgu_ln_g", 512), ("sgu_ln_b", 512)]:
        vec_d[nm] = nc.dram_tensor(nm, [L, n], F32, kind="ExternalInput").ap()
    sgu_w_d = nc.dram_tensor("sgu_w", [L, 8, 128, 128], F32, kind="ExternalInput").ap()
    sgu_b_d = nc.dram_tensor("sgu_b", [L, 8, 128], F32, kind="ExternalInput").ap()
    w_fi_d = nc.dram_tensor("w_ffn_in", [L, D, 2 * DFF], F32, kind="ExternalInput").ap()
    w_fo_d = nc.dram_tensor("w_ffn_out", [L, DFF, D], F32, kind="ExternalInput").ap()
    w_ple_d = nc.dram_tensor("w_ple", [L, 256, D], F32, kind="ExternalInput").ap()
    w_pg_d = nc.dram_tensor("w_ple_gate", [L, D, D], F32, kind="ExternalInput").ap()
    out_d = nc.dram_tensor("out", [S, D], F32, kind="ExternalOutput").ap()

    es = contextlib.ExitStack()
    with es:
        P = Prog(nc, es)

        def sb(name, shape, dt):
            return es.enter_context(nc.sbuf_tensor(name, shape, dt))

        PS = es.enter_context(nc.psum_tensor("ps", [128, 8, 512], F32))
        PB = [Buf("bank%d" % i) for i in range(8)]

        X = sb("X", [128, NT, D], F32)
        Xb = [Buf("X%d" % i) for i in range(NT)]
        cosT = sb("cosT", [128, NT, 8], F32)
        sinT = sb("sinT", [128, NT, 8], F32)
        b_trig = Buf("trig")
        ident = sb("ident", [128, 128], BF16)
        identf = sb("identf", [128, 128], F32)
        tri = sb("tri", [128, 128], BF16)
        E8 = sb("E8", [8, 512], BF16)
        b_const = Buf("const")
        ring = [sb("ring%d" % i, [128, 4096], BF16) for i in range(RING)]
        ringb = [[Buf("ring%d_%d" % (i, j)) for j in range(2)] for i in range(RING)]
        rings = [[P.dmasem() for j in range(2)] for i in range(RING)]
        ring_pos = [0]
        lng = sb("lng", [128, D], F32)
        lnb = sb("lnb", [128, D], F32)
        b_lng, b_lnb = Buf("lng"), Buf("lnb")
        s_lng, s_lnb = P.dmasem(), P.dmasem()
        sgg = sb("sgg", [128, 512], F32)
        sgb = sb("sgb", [128, 512], F32)
        b_sgg, b_sgb = Buf("sgg"), Buf("sgb")
        s_sgg, s_sgb = P.dmasem(), P.dmasem()
        WsT = sb("WsT", [128, 8, 128], BF16)
        b_WsT = Buf("WsT")
        sgub = sb("sgub", [8, 128], BF16)
        b_sgub = Buf("sgub")
        s_sgub = P.dmasem()
        kT = sb("kT", [128, S], BF16)
        kiT = sb("kiT", [128, S], BF16)
        V = sb("V", [128, NT, 2, 65], BF16)
        b_kT = [Buf("kT%d" % i) for i in range(NT)]
        b_kiT = [Buf("kiT%d" % i) for i in range(NT)]
        b_V = [Buf("V%d" % i) for i in range(NT)]
        xT = sb("xT", [128, 8, 512], BF16)
        b_xT = [Buf("xT%d" % i) for i in range(4)]
        xb = sb("xb", [128, D], BF16)
        b_xb = Buf("xb")
        QK = sb("QK", [128, 4, 16, 64], BF16)
        b_QK = [Buf("QK%d" % i) for i in range(4)]
        qT = sb("qT", [128, 4, 512], BF16)
        b_qT = [Buf("qT%d" % i) for i in range(4)]
        qiT = sb("qiT", [128, 2, 512], BF16)
        b_qiT = [Buf("qiT%d" % i) for i in range(4)]
        wi = sb("wi", [128, 4, 4], F32)
        b_wi = [Buf("wi%d" % i) for i in range(4)]
        bout = sb("bout", [128, 4, 512], BF16)
        b_bout = [Buf("bout%d" % i) for i in range(4)]
        rt = [sb("rt%d" % i, [128, 128], F32) for i in range(4)]
        b_rt = [Buf("rt%d" % i) for i in range(4)]
        arena = sb("arena", [128, 9216], F32)
        TOK = Buf("arena_tok")

        def carve(off, shape, dt):
            n = 1
            for d in shape[1:]:
                n *= d
            nb = n * (4 if dt == F32 else 2)
            ap = arena[:, off // 4:(off + nb) // 4]
            if dt != F32:
                ap = ap.bitcast(dt)
            if len(shape) == 3:
                ap = ap.rearrange("p (a b) -> p a b", a=shape[1])
            return ap
        score = carve(0, [128, S], F32)
        b_score = Buf("score", TOK)
        relu = [carve(8192 + 4096 * i, [128, 4, 512], BF16) for i in range(2)]
        b_relu = [Buf("relu%d" % i, TOK) for i in range(2)]
        mask = [carve(16384 + 4096 * i, [128, S], BF16) for i in range(2)]
        b_mask = [Buf("mask%d" % i, TOK) for i in range(2)]
        diagw = sb("diagw", [128, 4, 128], BF16)
        b_diagw = Buf("diagw")
        bis = sb("bis", [128, 8], F32)
        b_bis = Buf("bis")
        b_bis2, b_bmid, b_bcnt, b_bind = Buf("bis2"), Buf("bmid"), Buf("bcnt"), Buf("bind")
        mx8 = sb("mx8", [128, 8], F32)
        expP = [carve(24576 + 2048 * i, [128, 1024], BF16) for i in range(2)]
        b_expP = [Buf("expP%d" % i, TOK) for i in range(2)]
        Pm = [carve(28672 + 2048 * i, [128, 8, 128], BF16) for i in range(2)]
        b_Pm = [Buf("Pm%d" % i, TOK) for i in range(2)]
        rs = sb("rs", [128, 8], F32)
        b_rs = Buf("rs")
        cat = sb("cat", [128, D], BF16)
        b_cat = Buf("cat")
        catT = sb("catT", [128, 8, 128], BF16)
        b_catT = Buf("catT")
        lnst = sb("lnst", [128, 2, 6], F32)
        lnmv = sb("lnmv", [128, 4], F32)
        b_lnst = Buf("lnst")
        b_lnst2 = Buf("lnst2")
        b_lnmv = Buf("lnmv")
        gu = carve(32768, [128, 512], BF16)
        b_gu = Buf("gu", TOK)
        gv = carve(33792, [128, 512], F32)
        b_gv = Buf("gv", TOK)
        gvn = carve(35840, [128, 512], BF16)
        b_gvn = Buf("gvn", TOK)
        gT = carve(0, [128, NFC, 512], BF16)
        b_gT = [Buf("gT%d" % i, TOK) for i in range(NFC)]
        p_sb = carve(22528, [128, 4, 256], BF16)
        b_psb = Buf("p_sb", TOK)
        s_psb = P.dmasem()
        pT = carve(24576, [128, 2, 512], BF16)
        b_pT = Buf("pT", TOK)
        ffnT = [carve(26624 + 2048 * i, [128, 512], F32) for i in range(2)]
        b_ffnT = [Buf("ffnT%d" % i, TOK) for i in range(2)]
        silu = [carve(30720 + 1024 * i, [128, 512], BF16) for i in range(2)]
        b_silu = [Buf("silu%d" % i, TOK) for i in range(2)]
        sig = carve(32768, [128, D], F32)
        b_sig = Buf("sig", TOK)
        small = sb("small", [128, 16, 8], F32)
        small2 = sb("small2", [128, 16, 8], F32)
        posi = sb("posi", [128, NT], I32)
        posf = sb("posf", [128, NT], F32)
        invf = sb("invf", [128, 8], F32)
        halfpi = sb("halfpi", [128, 1], F32)
        s_pos = P.dmasem()
        b_pos = Buf("pos")
        s_x = [P.dmasem() for _ in range(NG)]
        s_out = [P.dmasem() for _ in range(NG)]

        def bank(i):
            return PS[:, i, :]

        def bankbf(i):
            return PS[:, i, :].bitcast(BF16)

        for g in range(NG):
            P.op("sp", lambda h, g=g: h.dma_start(
                out=X[:, 4 * g:4 * g + 4, :],
                in_=x_d[512 * g:512 * g + 512, :].rearrange("(n p) d -> p n d", p=128)),
                writes=Xb[4 * g:4 * g + 4], dma=s_x[g])
        P.op("sp", lambda h: h.dma_start(out=posi[:], in_=pos_d[:, :]), writes=[b_pos], dma=s_pos)

        def pool1(fn, reads=(), writes=()):
            P.op("pool", fn, reads=list(reads), writes=list(writes))
        bc = {k: Buf("c_" + k) for k in ["ident", "identf", "tri", "E8", "invf", "halfpi"]}
        pool1(lambda h: h.memset(ident[:], 1.0), writes=[bc["ident"]])
        pool1(lambda h: h.affine_select(out=ident[:], in_=ident[:], pattern=[[-1, 128]], compare_op=ALU.is_equal,
                                        fill=0.0, base=0, channel_multiplier=1), reads=[bc["ident"]], writes=[bc["ident"]])
        pool1(lambda h: h.memset(identf[:], 1.0), writes=[bc["identf"]])
        pool1(lambda h: h.affine_select(out=identf[:], in_=identf[:], pattern=[[-1, 128]], compare_op=ALU.is_equal,
                                        fill=0.0, base=0, channel_multiplier=1), reads=[bc["identf"]], writes=[bc["identf"]])
        pool1(lambda h: h.memset(tri[:], 1.0), writes=[bc["tri"]])
        pool1(lambda h: h.affine_select(out=tri[:], in_=tri[:], pattern=[[1, 128]], compare_op=ALU.is_ge,
                                        fill=0.0, base=0, channel_multiplier=-1), reads=[bc["tri"]], writes=[bc["tri"]])
        pool1(lambda h: h.memset(E8[:], 1.0), writes=[bc["E8"]])
        pool1(lambda h: h.affine_select(out=E8[:], in_=E8[:], pattern=[[1, 512]], compare_op=ALU.is_ge,
                                        fill=0.0, base=0, channel_multiplier=-64), reads=[bc["E8"]], writes=[bc["E8"]])
        pool1(lambda h: h.affine_select(out=E8[:], in_=E8[:], pattern=[[-1, 512]], compare_op=ALU.is_ge,
                                        fill=0.0, base=63, channel_multiplier=64), reads=[bc["E8"]], writes=[bc["E8"]])
        pool1(lambda h: h.memset(V[:], 1.0), writes=b_V)
        for i8 in range(8):
            pool1(lambda h, i8=i8: h.memset(invf[:, i8:i8 + 1], float(np.float32(500000.0) ** np.float32(-2.0 * i8 / 16))),
                  writes=[bc["invf"]])
        pool1(lambda h: h.memset(halfpi[:], math.pi / 2), writes=[bc["halfpi"]])
        pool1(lambda h: h.memset(small[:, 0, 0:1], 0.0), reads=list(bc.values()), writes=[b_const])

        twopi = 2 * math.pi
        c1 = 6.28125
        c2 = float(np.float32(twopi - c1))
        c2 = float(np.frombuffer(np.array([np.frombuffer(np.float32(c2).tobytes(), np.uint32)[0] & 0xFFFFF000],
                                          np.uint32).tobytes(), np.float32)[0])
        c3 = float(twopi - c1 - c2)
        MAGIC = 12582912.0

        b_s1, b_s2 = Buf("small"), Buf("small2")

        def dve1(fn, reads=(), writes=()):
            P.op("dve", fn, reads=list(reads), writes=list(writes))
        dve1(lambda h: h.tensor_copy(out=posf[:], in_=posi[:]), reads=[b_pos, b_const], writes=[b_trig])
        dve1(lambda h: h.tensor_tensor(out=small[:], in0=posf[:].unsqueeze(2).to_broadcast([128, NT, 8]),
                                       in1=invf[:].unsqueeze(1).to_broadcast([128, NT, 8]), op=ALU.mult),
             reads=[b_trig, b_const], writes=[b_s1])
        dve1(lambda h: h.tensor_scalar(out=small2[:], in0=small[:], scalar1=float(1.0 / twopi), scalar2=None, op0=ALU.mult),
             reads=[b_s1], writes=[b_s2])
        dve1(lambda h: h.tensor_scalar(out=small2[:], in0=small2[:], scalar1=MAGIC, scalar2=None, op0=ALU.add),
             reads=[b_s2], writes=[b_s2])
        dve1(lambda h: h.tensor_scalar(out=small2[:], in0=small2[:], scalar1=-MAGIC, scalar2=None, op0=ALU.add),
             reads=[b_s2], writes=[b_s2])
        for c in (c1, c2, c3):
            dve1(lambda h, c=c: h.scalar_tensor_tensor(out=small[:], in0=small2[:], scalar=-c, in1=small[:],
                                                       op0=ALU.mult, op1=ALU.add),
                 reads=[b_s1, b_s2], writes=[b_s1])
        dve1(lambda h: h.tensor_scalar(out=small[:], in0=small[:], scalar1=3.1415925, scalar2=-3.1415925,
                                       op0=ALU.min, op1=ALU.max), reads=[b_s1], writes=[b_s1])
        dve1(lambda h: h.tensor_scalar(out=small2[:], in0=small[:], scalar1=-1.0, scalar2=None, op0=ALU.mult),
             reads=[b_s1, b_s2], writes=[b_s2])
        dve1(lambda h: h.tensor_tensor(out=small2[:], in0=small[:], in1=small2[:], op=ALU.max),
             reads=[b_s1, b_s2], writes=[b_s2])
        b_trig2 = Buf("trig2")
        P.op("act", lambda h: h.activation(out=sinT[:], in_=small[:], func=AF.Sin), reads=[b_s1], writes=[b_trig2])
        P.op("act", lambda h: h.activation(out=cosT[:], in_=small2[:], func=AF.Sin, scale=-1.0, bias=halfpi[:]),
             reads=[b_s2, b_const], writes=[b_trig2])

        def ring_next():
            i = ring_pos[0] % RING
            ring_pos[0] += 1
            return i

        def load_w(slot, half, dst_ap, src_ap, split=False):
            P.op("pool", lambda h: h.dma_start(out=dst_ap, in_=src_ap),
                 writes=([ringb[slot][half]] if split else ringb[slot]), dma=rings[slot][half])

        def transposes_to(src_tile_ap_fn, n, bankid, reads):
            for c in range(n):
                P.op("pe", lambda h, c=c: h.transpose(out=bankbf(bankid)[:, c * 128:(c + 1) * 128],
                                                       in_=src_tile_ap_fn(c), identity=ident[:]),
                     reads=reads + [b_const], writes=[PB[bankid]])

        def layer_norm_inplace(i, gbuf, bbuf, g_ap, b_ap):
            xi = X[:, i, :]

            P.op("dve", lambda h: h.bn_stats(out=lnst[:, 0, :], in_=X[:, i, 0:512]), reads=[Xb[i]], writes=[b_lnst])
            P.op("dve", lambda h: h.bn_stats(out=lnst[:, 1, :], in_=X[:, i, 512:1024]), reads=[Xb[i]], writes=[b_lnst2])
            P.op("dve", lambda h: h.bn_aggr(out=lnmv[:, 0:2], in_=lnst[:].rearrange("p a b -> p (a b)")),
                 reads=[b_lnst, b_lnst2], writes=[b_lnmv])
            P.op("dve", lambda h: h.tensor_scalar(out=lnmv[:, 2:3], in0=lnmv[:, 1:2], scalar1=EPS, scalar2=None, op0=ALU.add),
                 reads=[b_lnmv], writes=[b_lnmv])
            P.op("act", lambda h: h.activation(out=lnmv[:, 2:3], in_=lnmv[:, 2:3], func=AF.Sqrt),
                 reads=[b_lnmv], writes=[b_lnmv])
            P.op("dve", lambda h: h.reciprocal(out=lnmv[:, 2:3], in_=lnmv[:, 2:3]), reads=[b_lnmv], writes=[b_lnmv])
            P.op("dve", lambda h: h.tensor_scalar(out=lnmv[:, 3:4], in0=lnmv[:, 0:1], scalar1=lnmv[:, 2:3], scalar2=-1.0,
                                                  op0=ALU.mult, op1=ALU.mult), reads=[b_lnmv], writes=[b_lnmv])
            P.op("act", lambda h: h.activation(out=xi, in_=xi, func=AF.Identity, scale=lnmv[:, 2:3], bias=lnmv[:, 3:4]),
                 reads=[b_lnmv, Xb[i]], writes=[Xb[i]])
            P.op("dve", lambda h: h.tensor_tensor(out=xi, in0=xi, in1=g_ap, op=ALU.mult),
                 reads=[Xb[i], gbuf], writes=[Xb[i]])
            P.op("dve", lambda h: h.tensor_tensor(out=xi, in0=xi, in1=b_ap, op=ALU.add),
                 reads=[Xb[i], bbuf], writes=[Xb[i]])

        def load_vec(dst, buf, sem, name, l):
            P.op("sp", lambda h: h.dma_start(out=dst[:], in_=vec_d[name][l, :].partition_broadcast(128)),
                 writes=[buf], dma=sem)

        def body_layers():
          for l in range(L):
              stage(1)
              load_vec(sgg, b_sgg, s_sgg, "sgu_ln_g", l)
              load_vec(sgb, b_sgb, s_sgb, "sgu_ln_b", l)
              sl = ring_next()
              load_w(sl, 0, ring[sl][:, 0:1024].rearrange("p (g s) -> p g s", g=8),
                     sgu_w_d[l].rearrange("g t s -> t g s"))
              for g8 in range(8):
                  P.op("pe", lambda h, g8=g8, sl=sl: h.transpose(out=bankbf(0)[:, g8 * 128:(g8 + 1) * 128],
                                                                 in_=ring[sl][:, g8 * 128:(g8 + 1) * 128], identity=ident[:]),
                       reads=ringb[sl] + [b_const], writes=[PB[0]])
              P.op("dve", lambda h: h.tensor_tensor(out=WsT[:], in0=bankbf(0).rearrange("p (g t) -> p g t", g=8),
                                                    in1=tri[:].unsqueeze(1).to_broadcast([128, 8, 128]), op=ALU.mult),
                   reads=[PB[0], b_const], writes=[b_WsT])
              P.op("pool", lambda h, l=l: h.dma_start(out=sgub[:], in_=sgu_b_d[l]), writes=[b_sgub], dma=s_sgub)

              for g in range(NG):
                  stage(2 + g * 10)
                  P.op("dve", lambda h: h.memset(mx8[:, 0:1], 0.0), writes=[TOK])
                  for tt in range(4):
                      i = 4 * g + tt
                      P.op("act", lambda h, i=i: h.activation(func=AF.Identity, out=xb[:], in_=X[:, i, :]), reads=[Xb[i]], writes=[b_xb])
                      tb = tt % 2
                      transposes_to(lambda c: xb[:, c * 128:(c + 1) * 128], 8, tb, [b_xb])
                      P.op("dve", lambda h, tt=tt, tb=tb: h.tensor_copy(
                          out=xT[:, :, tt * 128:(tt + 1) * 128], in_=bankbf(tb).rearrange("p (c t) -> p c t", c=8)),
                          reads=[PB[tb]], writes=[b_xT[tt]])
                  stage(2.1 + g * 10)
                  wsl = []
                  for ci in range(3):
                      c0, cw = CH[ci]
                      sl = ring_next()
                      wsl.append(sl)
                      load_w(sl, 0, ring[sl][:, 0:8 * cw].rearrange("p (c n) -> p c n", c=8),
                             w_in_d[l, :, c0:c0 + cw].rearrange("(c p) n -> p c n", p=128))
                  for tt in range(4):
                      i = 4 * g + tt
                      for ci in range(3):
                          c0, cw = CH[ci]
                          sl = wsl[ci]
                          bk = 2 + ci
                          wv = ring[sl][:, 0:8 * cw].rearrange("p (c n) -> p c n", c=8)
                          for kc in range(8):
                              P.op("pe", lambda h, kc=kc, bk=bk, wv=wv, cw=cw, tt=tt: h.matmul(
                                  bank(bk)[:, 0:cw], lhsT=xT[:, kc, tt * 128:(tt + 1) * 128], rhs=wv[:, kc, :],
                                  start=(kc == 0), stop=(kc == 7)),
                                  reads=[b_xT[tt]] + ringb[sl], writes=[PB[bk]])
                      stage(2.2 + g * 10)
                      bA = bank(2)
                      x1 = bA[:, 0:120].rearrange("p (h d) -> p h d", d=8)
                      x2 = bA[:, 120:240].rearrange("p (h d) -> p h d", d=8)
                      cb = cosT[:, i:i + 1, :].to_broadcast([128, 15, 8])
                      sbb = sinT[:, i:i + 1, :].to_broadcast([128, 15, 8])
                      t1 = rt[0][:, 0:120].rearrange("p (h d) -> p h d", d=8)
                      t2 = rt[1][:, 0:120].rearrange("p (h d) -> p h d", d=8)
                      t3 = rt[2][:, 0:120].rearrange("p (h d) -> p h d", d=8)
                      t4 = rt[3][:, 0:120].rearrange("p (h d) -> p h d", d=8)

                      def rope(h, x1=x1, x2=x2, cb=cb, sbb=sbb, tt=tt):
                          h.tensor_tensor(out=t1, in0=x1, in1=cb, op=ALU.mult)
                          h.tensor_tensor(out=t2, in0=x2, in1=sbb, op=ALU.mult)
                          h.tensor_tensor(out=t3, in0=x2, in1=cb, op=ALU.mult)
                          return h.tensor_tensor(out=t4, in0=x1, in1=sbb, op=ALU.mult)
                      P.op("dve", rope, reads=[PB[2], b_trig2], writes=b_rt)

                      def rope2(h, tt=tt):
                          h.tensor_tensor(out=QK[:, tt, 0:15, 0:8], in0=t1, in1=t2, op=ALU.subtract)
                          return h.tensor_tensor(out=QK[:, tt, 0:15, 8:16], in0=t3, in1=t4, op=ALU.add)
                      P.op("dve", rope2, reads=b_rt, writes=[b_QK[tt]])

                      def rope3(h, tt=tt):
                          h.tensor_tensor(out=QK[:, tt, 15, 0:8], in0=rt[0][:, 112:120], in1=rt[1][:, 112:120], op=ALU.subtract)
                          return h.tensor_tensor(out=QK[:, tt, 15, 8:16], in0=rt[2][:, 112:120], in1=rt[3][:, 112:120],
                                                 op=ALU.add)
                      if "b" in KDBG:
                          P.op("dve", rope3, reads=b_rt, writes=[b_QK[tt]])
                      if "c" in KDBG:
                          P.op("act", lambda h, tt=tt, bA=bA: h.activation(func=AF.Identity, out=QK[:, tt, 15, 16:64],
                                                                           in_=bA[:, 436:484]),
                               reads=[PB[2]], writes=[b_QK[tt]])
                      stage(2.3 + g * 10)
                      P.op("dve", lambda h, tt=tt, bA=bA: h.tensor_scalar(out=wi[:, tt, :], in0=bA[:, 240:244], scalar1=1.0 / 16,
                                                                          scalar2=None, op0=ALU.mult),
                           reads=[PB[2]], writes=[b_wi[tt]])
                      stage(2.3 + 0.01 * 1 + g * 10)
                      P.op("act", lambda h, tt=tt, bA=bA: h.activation(func=AF.Identity, out=QK[:, tt, 10:15, 16:64],
                                                                 in_=bA[:, 244:484].rearrange("p (h d) -> p h d", d=48)),
                           reads=[PB[2]], writes=[b_QK[tt]])
                      stage(2.3 + 0.01 * 2 + g * 10)
                      P.op("act", lambda h, tt=tt: h.activation(func=AF.Identity, out=QK[:, tt, 0:10, 16:64],
                                                          in_=bank(3)[:, 0:480].rearrange("p (h d) -> p h d", d=48)),
                           reads=[PB[3]], writes=[b_QK[tt]])
                      stage(2.3 + 0.01 * 3 + g * 10)
                      P.op("act", lambda h, i=i: h.activation(func=AF.Identity, out=V[:, i, :, 0:64],
                                                        in_=bank(4)[:, 0:128].rearrange("p (g d) -> p g d", d=64)),
                           reads=[PB[4]], writes=[b_V[i]])
                      stage(2.3 + 0.01 * 4 + g * 10)
                      stage(2.4 + g * 10)
                      tb = tt % 2
                      qkf = QK[:, tt, :, :].rearrange("p s d -> p (s d)")
                      transposes_to(lambda c, qkf=qkf: qkf[:, c * 128:(c + 1) * 128], 8, tb, [b_QK[tt]])
                      bb = bankbf(tb).rearrange("p (c t) -> p c t", c=8)
                      P.op("dve", lambda h, tt=tt, bb=bb: h.tensor_copy(out=qT[:, :, tt * 128:(tt + 1) * 128], in_=bb[:, 0:4, :]),
                           reads=[PB[tb]], writes=[b_qT[tt]])
                      P.op("dve", lambda h, i=i, bb=bb: h.tensor_copy(out=kT[:, i * 128:(i + 1) * 128], in_=bb[:, 4, :]),
                           reads=[PB[tb]], writes=[b_kT[i]])
                      P.op("dve", lambda h, tt=tt, bb=bb: h.tensor_copy(out=qiT[:, :, tt * 128:(tt + 1) * 128], in_=bb[:, 5:7, :]),
                           reads=[PB[tb]], writes=[b_qiT[tt]])
                      P.op("dve", lambda h, i=i, bb=bb: h.tensor_copy(out=kiT[:, i * 128:(i + 1) * 128], in_=bb[:, 7, :]),
                           reads=[PB[tb]], writes=[b_kiT[i]])

                  stage(3 + g * 10)
                  sD, sE = ring_next(), ring_next()
                  for sl, ci in ((sD, 3), (sE, 4)):
                      c0, cw = CH[ci]
                      load_w(sl, 0, ring[sl][:, 0:8 * cw].rearrange("p (c n) -> p c n", c=8),
                             w_in_d[l, :, c0:c0 + cw].rearrange("(c p) n -> p c n", p=128))
                  for tt in range(4):
                      i = 4 * g + tt
                      for sl, bk in ((sD, 5), (sE, 6)):
                          wv = ring[sl][:].rearrange("p (c n) -> p c n", c=8)
                          for kc in range(8):
                              P.op("pe", lambda h, kc=kc, bk=bk, wv=wv, tt=tt: h.matmul(
                                  bank(bk), lhsT=xT[:, kc, tt * 128:(tt + 1) * 128], rhs=wv[:, kc, :],
                                  start=(kc == 0), stop=(kc == 7)),
                                  reads=[b_xT[tt]] + ringb[sl], writes=[PB[bk]])
                      P.op("act", lambda h: h.activation(out=gu[:], in_=bank(5), func=AF.Gelu_apprx_tanh),
                           reads=[PB[5]], writes=[b_gu])
                      P.op("act", lambda h: h.activation(out=gv[:], in_=bank(6), func=AF.Gelu_apprx_tanh),
                           reads=[PB[6]], writes=[b_gv])

                      P.op("dve", lambda h: h.bn_stats(out=lnst[:, 0, :], in_=gv[:]), reads=[b_gv], writes=[b_lnst])
                      P.op("dve", lambda h: h.bn_aggr(out=lnmv[:, 0:2], in_=lnst[:, 0, :]), reads=[b_lnst], writes=[b_lnmv])
                      P.op("dve", lambda h: h.tensor_scalar(out=lnmv[:, 2:3], in0=lnmv[:, 1:2], scalar1=EPS, scalar2=None,
                                                            op0=ALU.add), reads=[b_lnmv], writes=[b_lnmv])
                      P.op("act", lambda h: h.activation(out=lnmv[:, 2:3], in_=lnmv[:, 2:3], func=AF.Sqrt),
                           reads=[b_lnmv], writes=[b_lnmv])
                      P.op("dve", lambda h: h.reciprocal(out=lnmv[:, 2:3], in_=lnmv[:, 2:3]), reads=[b_lnmv], writes=[b_lnmv])
                      P.op("dve", lambda h: h.tensor_scalar(out=lnmv[:, 3:4], in0=lnmv[:, 0:1], scalar1=lnmv[:, 2:3],
                                                            scalar2=-1.0, op0=ALU.mult, op1=ALU.mult),
                           reads=[b_lnmv], writes=[b_lnmv])
                      P.op("act", lambda h: h.activation(out=gv[:], in_=gv[:], func=AF.Identity, scale=lnmv[:, 2:3],
                                                         bias=lnmv[:, 3:4]),
                           reads=[b_lnmv, b_gv], writes=[b_gv])
                      P.op("dve", lambda h: h.tensor_tensor(out=gv[:], in0=gv[:], in1=sgg[:], op=ALU.mult),
                           reads=[b_gv, b_sgg], writes=[b_gv])
                      P.op("dve", lambda h: h.tensor_tensor(out=gvn[:], in0=gv[:], in1=sgb[:], op=ALU.add),
                           reads=[b_gv, b_sgb], writes=[b_gvn])
                      for g8 in range(8):
                          P.op("pe", lambda h, g8=g8: h.matmul(bank(7)[:, g8 * 64:(g8 + 1) * 64], lhsT=WsT[:, g8, :],
                                                               rhs=gvn[:, g8 * 64:(g8 + 1) * 64],
                                                               start=(g8 == 0), stop=False, skip_group_check=True),
                               reads=[b_WsT, b_gvn], writes=[PB[7]])
                      P.op("pe", lambda h: h.matmul(bank(7), lhsT=sgub[:], rhs=E8[:], start=False, stop=True,
                                                    skip_group_check=True),
                           reads=[b_sgub, b_const], writes=[PB[7]])
                      P.op("dve", lambda h, tt=tt: h.tensor_tensor(out=bout[:, tt, :], in0=gu[:], in1=bank(7), op=ALU.mult),
                           reads=[b_gu, PB[7]], writes=[b_bout[tt]])

                  stage(4 + g * 10)
                  sO = [ring_next(), ring_next()]
                  for hh in range(2):
                      load_w(sO[hh], 0, ring[sO[hh]][:].rearrange("p (c n) -> p c n", c=8),
                             w_o_d[l, :, hh * 512:(hh + 1) * 512].rearrange("(c p) n -> p c n", p=128))
                  load_vec(lng, b_lng, s_lng, "ln1_g", l)
                  load_vec(lnb, b_lnb, s_lnb, "ln1_b", l)

                  for tt in range(4):
                      i = 4 * g + tt
                      N = 128 * (i + 1)
                      mk = mask[i % 2]
                      bmk = b_mask[i % 2]
                      if i < 2:
                          P.op("dve", lambda h, mk=mk, N=N: h.memset(mk[:, 0:N], 1.0), writes=[bmk])
                          P.op("dve", lambda h, mk=mk, N=N: h.memset(mk[0:64, N - 64:N], 0.0), reads=[bmk], writes=[bmk])
                      else:
                          def dg(h, tt=tt):
                              for hh in range(4):
                                  r = h.tensor_scalar(out=diagw[:, hh, :], in0=ident[:], scalar1=wi[:, tt, hh:hh + 1],
                                                      scalar2=None, op0=ALU.mult)
                              return r
                          P.op("dve", dg, reads=[b_wi[tt], b_const], writes=[b_diagw])
                          ncs = (N + 511) // 512
                          for c in range(ncs):
                              wc = min(512, N - 512 * c)
                              rb = c % 2
                              for hh in range(4):
                                  pr = (hh % 2) * 64
                                  P.op("pe", lambda h, hh=hh, pr=pr, c=c, wc=wc, tt=tt: h.matmul(
                                      bank(hh)[:, 0:wc], lhsT=qiT[pr:pr + 64, hh // 2, tt * 128:(tt + 1) * 128],
                                      rhs=kiT[pr:pr + 64, c * 512:c * 512 + wc], start=True, stop=True),
                                      reads=[b_qiT[tt]] + b_kiT[4 * c:4 * c + 4], writes=[PB[hh]])
                              P.op("act", lambda h, rb=rb, wc=wc: h.activation(out=relu[rb][:, :, 0:wc], in_=PS[:, 0:4, 0:wc],
                                                                               func=AF.Relu),
                                   reads=PB[0:4], writes=[b_relu[rb]])
                              sbk = 4 + (c % 2)
                              for hh in range(4):
                                  P.op("pe", lambda h, hh=hh, rb=rb, wc=wc, sbk=sbk: h.matmul(
                                      bank(sbk)[:, 0:wc], lhsT=diagw[:, hh, :], rhs=relu[rb][:, hh, 0:wc],
                                      start=(hh == 0), stop=(hh == 3)),
                                      reads=[b_diagw, b_relu[rb]], writes=[PB[sbk]])
                              P.op("dve", lambda h, c=c, wc=wc, sbk=sbk: h.tensor_copy(out=score[:, c * 512:c * 512 + wc],
                                                                                       in_=bank(sbk)[:, 0:wc]),
                                   reads=[PB[sbk]], writes=[b_score])

                          def B1(fn, reads=(), writes=()):
                              P.op("dve", fn, reads=list(reads), writes=list(writes))
                          B1(lambda h, N=N: h.tensor_reduce(out=bis[:, 0:1], in_=score[:, 0:N], axis=AX.X, op=ALU.min),
                             reads=[b_score], writes=[b_bis])
                          B1(lambda h, N=N: h.tensor_reduce(out=bis[:, 5:6], in_=score[:, 0:N], axis=AX.X, op=ALU.max),
                             reads=[b_score], writes=[b_bis2])
                          B1(lambda h, N=N: h.memset(score[0:64, N - 64:N], -BIG), reads=[b_bis, b_bis2], writes=[b_score])
                          B1(lambda h: h.tensor_tensor(out=bis[:, 1:2], in0=bis[:, 5:6], in1=bis[:, 0:1], op=ALU.subtract),
                             reads=[b_bis, b_bis2], writes=[b_bis2])
                          B1(lambda h: h.tensor_scalar(out=bis[:, 1:2], in0=bis[:, 1:2], scalar1=1.0001, scalar2=1e-6,
                                                       op0=ALU.mult, op1=ALU.add), reads=[b_bis2], writes=[b_bis2])
                          for it in range(1, NBIS + 1):
                              f = 2.0 ** (-it)
                              B1(lambda h, f=f: h.scalar_tensor_tensor(out=bis[:, 2:3], in0=bis[:, 1:2], scalar=f,
                                                                       in1=bis[:, 0:1], op0=ALU.mult, op1=ALU.add),
                                 reads=[b_bis, b_bis2], writes=[b_bmid])
                              B1(lambda h, N=N, mk=mk: h.tensor_scalar(out=mk[:, 0:N], in0=score[:, 0:N], scalar1=bis[:, 2:3],
                                                                       scalar2=None, op0=ALU.is_ge, op1=ALU.add,
                                                                       accum_out=bis[:, 3:4]),
                                 reads=[b_score, b_bmid], writes=[b_bcnt, bmk])
                              B1(lambda h, f=f: h.tensor_scalar(out=bis[:, 4:5], in0=bis[:, 3:4], scalar1=TOPK - 0.5, scalar2=f,
                                                                op0=ALU.is_ge, op1=ALU.mult),
                                 reads=[b_bcnt], writes=[b_bind])
                              B1(lambda h: h.scalar_tensor_tensor(out=bis[:, 0:1], in0=bis[:, 4:5], scalar=bis[:, 1:2],
                                                                  in1=bis[:, 0:1], op0=ALU.mult, op1=ALU.add),
                                 reads=[b_bind, b_bis2, b_bis], writes=[b_bis])
                          P.op("dve", lambda h, N=N, mk=mk: h.tensor_scalar(out=mk[:, 0:N], in0=score[:, 0:N],
                                                                            scalar1=bis[:, 0:1], scalar2=None, op0=ALU.is_ge),
                               reads=[b_score, b_bis], writes=[bmk])

                      for j in range(i + 1):
                          pb0 = 0 if j % 2 == 0 else 2
                          eb = j % 2
                          for jj in range(4):
                              for grp in range(2):
                                  pr = grp * 64
                                  P.op("pe", lambda h, jj=jj, grp=grp, pr=pr, j=j, tt=tt, pb0=pb0: h.matmul(
                                      bank(pb0 + grp)[:, jj * 128:(jj + 1) * 128],
                                      lhsT=kT[pr:pr + 64, j * 128:(j + 1) * 128],
                                      rhs=qT[pr:pr + 64, jj, tt * 128:(tt + 1) * 128], start=True, stop=True),
                                      reads=[b_kT[j], b_qT[tt]], writes=[PB[pb0 + grp]])
                          P.op("act", lambda h, pb0=pb0, eb=eb: h.activation(
                              out=expP[eb][:].rearrange("p (b n) -> p b n", b=2), in_=PS[:, pb0:pb0 + 2, :],
                              func=AF.Exp, scale=0.125),
                              reads=[PB[pb0], PB[pb0 + 1]], writes=[b_expP[eb]])
                          P.op("pe", lambda h, mk=mk, j=j: h.transpose(out=bankbf(6)[:, 0:128], in_=mk[:, j * 128:(j + 1) * 128],
                                                                       identity=ident[:]),
                               reads=[bmk, b_const], writes=[PB[6]])
                          P.op("dve", lambda h, eb=eb: h.tensor_tensor(
                              out=Pm[eb][:], in0=expP[eb][:].rearrange("p (a t) -> p a t", a=8),
                              in1=bankbf(6)[:, 0:128].unsqueeze(1).to_broadcast([128, 8, 128]), op=ALU.mult),
                              reads=[b_expP[eb], PB[6]], writes=[b_Pm[eb]])
                          for grp in range(2):
                              for jj in range(4):
                                  P.op("pe", lambda h, grp=grp, jj=jj, j=j, i=i, eb=eb: h.matmul(
                                      bank(4 + grp)[:, jj * 65:(jj + 1) * 65], lhsT=Pm[eb][:, grp * 4 + jj, :],
                                      rhs=V[:, j, grp, :], start=(j == 0 and jj == 0), stop=(j == i and jj == 3),
                                      skip_group_check=True),
                                      reads=[b_Pm[eb], b_V[j]], writes=[PB[4 + grp]])
                      pv = PS[:, 4:6, 0:260].rearrange("p b (h e) -> p b h e", e=65)
                      P.op("dve", lambda h, pv=pv: h.reciprocal(out=rs[:].rearrange("p (b h) -> p b h", b=2), in_=pv[:, :, :, 64]),
                           reads=[PB[4], PB[5]], writes=[b_rs])
                      for grp in range(2):
                          P.op("dve", lambda h, pv=pv, grp=grp: h.tensor_tensor(
                              out=cat[:, grp * 256:(grp + 1) * 256].rearrange("p (h d) -> p h d", d=64),
                              in0=pv[:, grp, :, 0:64],
                              in1=rs[:, grp * 4:(grp + 1) * 4].unsqueeze(2).to_broadcast([128, 4, 64]), op=ALU.mult),
                              reads=[PB[4 + grp], b_rs], writes=[b_cat])
                      P.op("act", lambda h, tt=tt: h.activation(func=AF.Identity, out=cat[:, 512:1024], in_=bout[:, tt, :]),
                           reads=[b_bout[tt]], writes=[b_cat])
                      transposes_to(lambda c: cat[:, c * 128:(c + 1) * 128], 8, 7, [b_cat])
                      P.op("dve", lambda h: h.tensor_copy(out=catT[:], in_=bankbf(7).rearrange("p (c t) -> p c t", c=8)),
                           reads=[PB[7]], writes=[b_catT])
                      for hh in range(2):
                          wv = ring[sO[hh]][:].rearrange("p (c n) -> p c n", c=8)
                          for kc in range(8):
                              P.op("pe", lambda h, hh=hh, kc=kc, wv=wv: h.matmul(
                                  bank(hh), lhsT=catT[:, kc, :], rhs=wv[:, kc, :], start=(kc == 0), stop=(kc == 7)),
                                  reads=[b_catT] + ringb[sO[hh]], writes=[PB[hh]])
                      P.op("dve", lambda h, i=i: h.scalar_tensor_tensor(
                          out=X[:, i, :].rearrange("p (b n) -> p b n", b=2), in0=X[:, i, :].rearrange("p (b n) -> p b n", b=2),
                          scalar=ALPHA, in1=PS[:, 0:2, :], op0=ALU.mult, op1=ALU.add),
                          reads=[Xb[i], PB[0], PB[1]], writes=[Xb[i]])
                      layer_norm_inplace(i, b_lng, b_lnb, lng[:], lnb[:])

                  stage(6 + g * 10)
                  P.op("dve", lambda h: h.memset(mx8[:, 0:1], 0.0), writes=[TOK])
                  for tt in range(4):
                      i = 4 * g + tt
                      P.op("act", lambda h, i=i: h.activation(func=AF.Identity, out=xb[:], in_=X[:, i, :]), reads=[Xb[i]], writes=[b_xb])
                      tb = tt % 2
                      transposes_to(lambda c: xb[:, c * 128:(c + 1) * 128], 8, tb, [b_xb])
                      P.op("dve", lambda h, tt=tt, tb=tb: h.tensor_copy(
                          out=xT[:, :, tt * 128:(tt + 1) * 128], in_=bankbf(tb).rearrange("p (c t) -> p c t", c=8)),
                          reads=[PB[tb]], writes=[b_xT[tt]])
                  P.op("pool", lambda h, l=l, g=g: h.dma_start(
                      out=p_sb[:], in_=p_d[l, 512 * g:512 * g + 512, :].rearrange("(n p) d -> p n d", p=128)),
                      writes=[b_psb], dma=s_psb)
                  for tt in range(4):
                      for c in range(2):
                          P.op("pe", lambda h, tt=tt, c=c: h.transpose(
                              out=bankbf(2)[:, (tt * 2 + c) * 128:(tt * 2 + c + 1) * 128],
                              in_=p_sb[:, tt, c * 128:(c + 1) * 128], identity=ident[:]),
                              reads=[b_psb, b_const], writes=[PB[2]])
                  P.op("dve", lambda h: h.tensor_copy(out=pT[:].rearrange("p c (t n) -> p t c n", t=4),
                                               in_=bankbf(2).rearrange("p (t c n) -> p t c n", t=4, c=2)),
                       reads=[PB[2]], writes=[b_pT])
                  for fc in range(NFC):
                      sl = ring_next()
                      wv = ring[sl][:, 0:2048].rearrange("p (c n) -> p c n", c=8)
                      load_w(sl, 0, wv[:, :, 0:128], w_fi_d[l, :, fc * 128:(fc + 1) * 128].rearrange("(c p) n -> p c n", p=128),
                             split=True)
                      load_w(sl, 1, wv[:, :, 128:256],
                             w_fi_d[l, :, DFF + fc * 128:DFF + (fc + 1) * 128].rearrange("(c p) n -> p c n", p=128), split=True)
                      gb = 3 + 2 * (fc % 2)
                      for hf in range(2):
                          for kc in range(8):
                              P.op("pe", lambda h, hf=hf, kc=kc, wv=wv, gb=gb: h.matmul(
                                  bank(gb + hf), lhsT=wv[:, kc, hf * 128:(hf + 1) * 128], rhs=xT[:, kc, :],
                                  start=(kc == 0), stop=(kc == 7)),
                                  reads=b_xT + [ringb[sl][hf]], writes=[PB[gb + hf]])
                      sb_ = fc % 2
                      P.op("act", lambda h, gb=gb, sb_=sb_: h.activation(out=silu[sb_][:], in_=bank(gb), func=AF.Silu),
                           reads=[PB[gb]], writes=[b_silu[sb_]])
                      P.op("dve", lambda h, gb=gb, sb_=sb_, fc=fc: h.tensor_tensor(out=gT[:, fc, :], in0=silu[sb_][:],
                                                                                   in1=bank(gb + 1), op=ALU.mult),
                           reads=[b_silu[sb_], PB[gb + 1]], writes=[b_gT[fc]])
                  for n8 in range(8):
                      sl = ring_next()
                      wv = ring[sl][:, 0:NFC * 128].rearrange("p (c n) -> p c n", c=NFC)
                      load_w(sl, 0, wv, w_fo_d[l, :, n8 * 128:(n8 + 1) * 128].rearrange("(c p) n -> p c n", p=128))
                      ob = n8 % 2
                      for fc in range(NFC):
                          P.op("pe", lambda h, fc=fc, wv=wv, ob=ob: h.matmul(
                              bank(ob), lhsT=wv[:, fc, :], rhs=gT[:, fc, :], start=(fc == 0), stop=(fc == NFC - 1)),
                              reads=[b_gT[fc]] + ringb[sl], writes=[PB[ob]])
                      P.op("act", lambda h, ob=ob: h.activation(func=AF.Identity, out=ffnT[ob][:], in_=bank(ob)), reads=[PB[ob]], writes=[b_ffnT[ob]])
                      for tt in range(4):
                          P.op("pe", lambda h, tt=tt, ob=ob: h.transpose(out=bank(2)[:, tt * 128:(tt + 1) * 128],
                                                                         in_=ffnT[ob][:, tt * 128:(tt + 1) * 128],
                                                                         identity=identf[:]),
                               reads=[b_ffnT[ob], b_const], writes=[PB[2]])
                      P.op("dve", lambda h, n8=n8, g=g: h.scalar_tensor_tensor(
                          out=X[:, 4 * g:4 * g + 4, n8 * 128:(n8 + 1) * 128], in0=X[:, 4 * g:4 * g + 4, n8 * 128:(n8 + 1) * 128],
                          scalar=ALPHA, in1=bank(2).rearrange("p (t n) -> p t n", t=4), op0=ALU.mult, op1=ALU.add),
                          reads=[PB[2]] + Xb[4 * g:4 * g + 4], writes=Xb[4 * g:4 * g + 4])
                  stage(7 + g * 10)
                  sP = ring_next()
                  wpl = ring[sP][:, 0:2048].rearrange("p (c n) -> p c n", c=2)
                  load_w(sP, 0, wpl, w_ple_d[l].rearrange("(c p) n -> p c n", p=128))
                  sG = [ring_next(), ring_next()]
                  for hh in range(2):
                      load_w(sG[hh], 0, ring[sG[hh]][:].rearrange("p (c n) -> p c n", c=8),
                             w_pg_d[l, :, hh * 512:(hh + 1) * 512].rearrange("(c p) n -> p c n", p=128))
                  load_vec(lng, b_lng, s_lng, "ln2_g", l)
                  load_vec(lnb, b_lnb, s_lnb, "ln2_b", l)
                  for tt in range(4):
                      i = 4 * g + tt
                      for hh in range(2):
                          for c in range(2):
                              P.op("pe", lambda h, hh=hh, c=c, tt=tt, wpl=wpl: h.matmul(
                                  bank(4 + hh), lhsT=pT[:, c, tt * 128:(tt + 1) * 128], rhs=wpl[:, c, hh * 512:(hh + 1) * 512],
                                  start=(c == 0), stop=(c == 1)),
                                  reads=[b_pT] + ringb[sP], writes=[PB[4 + hh]])
                          wv = ring[sG[hh]][:].rearrange("p (c n) -> p c n", c=8)
                          for kc in range(8):
                              P.op("pe", lambda h, hh=hh, kc=kc, wv=wv, tt=tt: h.matmul(
                                  bank(6 + hh), lhsT=xT[:, kc, tt * 128:(tt + 1) * 128], rhs=wv[:, kc, :],
                                  start=(kc == 0), stop=(kc == 7)),
                                  reads=[b_xT[tt]] + ringb[sG[hh]], writes=[PB[6 + hh]])
                      P.op("act", lambda h: h.activation(out=sig[:].rearrange("p (b n) -> p b n", b=2), in_=PS[:, 6:8, :],
                                                         func=AF.Sigmoid),
                           reads=[PB[6], PB[7]], writes=[b_sig])
                      P.op("dve", lambda h: h.tensor_tensor(out=sig[:].rearrange("p (b n) -> p b n", b=2),
                                                            in0=sig[:].rearrange("p (b n) -> p b n", b=2),
                                                            in1=PS[:, 4:6, :], op=ALU.mult),
                           reads=[b_sig, PB[4], PB[5]], writes=[b_sig])
                      P.op("dve", lambda h, i=i: h.tensor_tensor(out=X[:, i, :], in0=X[:, i, :], in1=sig[:], op=ALU.add),
                           reads=[b_sig, Xb[i]], writes=[Xb[i]])
                      layer_norm_inplace(i, b_lng, b_lnb, lng[:], lnb[:])

        try:
            body_layers()
        except _Stop:
            pass
        for g in range(NG):
            P.op("sp", lambda h, g=g: h.dma_start(
                out=out_d[512 * g:512 * g + 512, :].rearrange("(n p) d -> p n d", p=128),
                in_=X[:, 4 * g:4 * g + 4, :]),
                reads=Xb[4 * g:4 * g + 4], dma=s_out[g])
        fin = Buf("fin")
        for g in range(NG):
            pass
        last = [len(P.ops) - NG + g for g in range(NG)]
        idx = P.op("sp", lambda h: None)
        P.ops[idx][2].update(last)
        P.emit()
    return nc


def prep_inputs(inputs, n_layers, b):
    perm = w_in_perm()
    L = n_layers
    m = {}
    m["x"] = np.ascontiguousarray(inputs["x"][b], dtype=np.float32)
    m["p"] = np.ascontiguousarray(inputs["p"][:L, b], dtype=np.float32)
    m["pos"] = np.ascontiguousarray(np.asarray(inputs["positions"][b]).reshape(NT, 128).T.astype(np.int32))
    return m


def shared_inputs(inputs, n_layers, l0=0):
    perm = w_in_perm()
    L = n_layers
    sl = slice(l0, l0 + L)
    m = {}
    m["w_in"] = np.ascontiguousarray(np.asarray(inputs["w_in"])[sl][:, :, perm], dtype=np.float32)
    for k in ["w_o", "ln1_g", "ln1_b", "ln2_g", "ln2_b", "sgu_ln_g", "sgu_ln_b", "sgu_w", "sgu_b",
              "w_ffn_in", "w_ffn_out", "w_ple", "w_ple_gate"]:
        m[k] = np.ascontiguousarray(np.asarray(inputs[k])[sl], dtype=np.float32)
    return m


_NC_CACHE = {}


def kernel(**inputs):
    inputs = {k: np.asarray(v) for k, v in inputs.items()}
    L = 4
    if L not in _NC_CACHE:
        _NC_CACHE[L] = build(L)
    nc = _NC_CACHE[L]
    sh = shared_inputs(inputs, L)
    in_maps = []
    for b in range(8):
        m = prep_inputs(inputs, L, b)
        m.update(sh)
        in_maps.append(m)
    res = run_bass_kernel_spmd(nc, in_maps, core_ids=list(range(8)))
    out = np.stack([np.asarray(r["out"], dtype=np.float32) for r in res.results], axis=0)
    return out
```

```python
import math
import numpy as np
import concourse.bass as bass
import concourse.mybir as mybir
from concourse.bass_utils import run_bass_kernel_spmd

F32 = mybir.dt.float32
BF16 = mybir.dt.bfloat16
I32 = mybir.dt.int32
AF = mybir.ActivationFunctionType
ALU = mybir.AluOpType
AX = mybir.AxisListType

S = 2048
D = 1024
NT = 16
NG = 4
DFF = 2816
NFC = 22
INC = 2116
ALPHA = 8 ** 0.25
EPS = 1e-5
TOPK = 256
NBIS = 22
BIG = 1.0e30
SAME_ENG_SYNC = True
RING = 4
import os
KDBG = os.environ.get('KDBG', 'bc')

CH = [(0, 484), (484, 480), (964, 128), (1092, 512), (1604, 512)]


def _slot_cols():
    q_order = [0, 4, 1, 5, 2, 6, 3, 7]
    bases = [h * 64 for h in q_order] + [512, 576] + [768 + 64 * h for h in range(4)] + [1024]
    return bases


def w_in_perm():
    bases = _slot_cols()
    cols = []
    for b in bases:
        cols += [b + d for d in range(8)]
    for b in bases:
        cols += [b + 8 + d for d in range(8)]
    cols += [1088 + h for h in range(4)]
    for s in range(10, 15):
        cols += [bases[s] + d for d in range(16, 64)]
    for s in range(0, 10):
        cols += [bases[s] + d for d in range(16, 64)]
    cols += list(range(640, 768))
    cols += list(range(1092, 2116))
    assert len(cols) == INC and len(set(cols)) == INC
    return np.array(cols)


class Buf:
    __slots__ = ("name", "lastw", "readers", "tok")

    def __init__(self, name, tok=None):
        self.name = name
        self.lastw = None
        self.readers = []
        self.tok = tok


class DmaSem:
    def __init__(self, sem):
        self.sem = sem
        self.count = 0


class Prog:
    ENGS = ["pe", "act", "dve", "pool", "sp"]

    def __init__(self, nc, es):
        self.nc = nc
        self.es = es
        self.ops = []
        self.sems = {e: es.enter_context(nc.semaphore("sem_" + e)) for e in self.ENGS}
        self.ndma = 0

    def dmasem(self):
        self.ndma += 1
        return DmaSem(self.es.enter_context(self.nc.semaphore("dsem%d" % self.ndma)))

    def op(self, eng, fn, reads=(), writes=(), dma=None):
        idx = len(self.ops)
        deps = set()
        toks = set(b.tok for b in list(reads) + list(writes) if b.tok is not None)
        if toks:
            reads = list(reads) + [t for t in toks if t not in writes]
        for b in reads:
            if b.lastw is not None:
                deps.add(b.lastw)
        for b in writes:
            if b.lastw is not None:
                deps.add(b.lastw)
            deps.update(b.readers)
        for b in reads:
            b.readers.append(idx)
        for b in writes:
            b.lastw = idx
            b.readers = []
        deps.discard(idx)
        self.ops.append([eng, fn, deps, dma, False, None])
        return idx

    def emit(self):
        nc = self.nc
        ops = self.ops
        for o in ops:
            eng, fn, deps, dma, _, _ = o
            for d in deps:
                p = ops[d]
                if p[3] is None and dma is None and p[0] == eng and (eng == "pe" or not SAME_ENG_SYNC):
                    continue
                p[4] = True
        cnt = {e: 0 for e in self.ENGS}
        for o in ops:
            if o[3] is not None:
                o[3].count += 16
                o[5] = (o[3].sem, o[3].count)
            elif o[4]:
                cnt[o[0]] += 1
                o[5] = (self.sems[o[0]], cnt[o[0]])
        per = {e: [] for e in self.ENGS}
        for o in ops:
            per[o[0]].append(o)
        names = {"pe": "tensor", "act": "scalar", "dve": "vector", "pool": "gpsimd", "sp": "sync"}
        with nc.Block() as block:
            for e in self.ENGS:
                def body(h, e=e):
                    waited = {}
                    for o in per[e]:
                        for d in sorted(o[2]):
                            p = ops[d]
                            if p[5] is None:
                                continue
                            if p[3] is None and o[3] is None and p[0] == e and (e == "pe" or not SAME_ENG_SYNC):
                                continue
                            sem, val = p[5]
                            key = id(sem)
                            if waited.get(key, 0) >= val:
                                continue
                            h.wait_ge(sem, val)
                            waited[key] = val
                        ins = o[1](h)
                        if o[5] is not None and ins is not None:
                            ins.then_inc(o[5][0], 16 if o[3] is not None else 1)
                getattr(block, names[e])(body)


class _Stop(Exception):
    pass


def build(n_layers, dbg=False, stage_limit=10 ** 9):
    import contextlib

    def stage(n):
        if n > stage_limit:
            raise _Stop()
    nc = bass.Bass("TRN2", target_bir_lowering=False)
    L = n_layers
    x_d = nc.dram_tensor("x", [S, D], F32, kind="ExternalInput").ap()
    p_d = nc.dram_tensor("p", [L, S, 256], F32, kind="ExternalInput").ap()
    pos_d = nc.dram_tensor("pos", [128, NT], I32, kind="ExternalInput").ap()
    w_in_d = nc.dram_tensor("w_in", [L, D, INC], F32, kind="ExternalInput").ap()
    w_o_d = nc.dram_tensor("w_o", [L, D, D], F32, kind="ExternalInput").ap()
    vec_d = {}
    for nm, n in [("ln1_g", D), ("ln1_b", D), ("ln2_g", D), ("ln2_b", D), ("sgu_ln_g", 512), ("sgu_ln_b", 512)]:
        vec_d[nm] = nc.dram_tensor(nm, [L, n], F32, kind="ExternalInput").ap()
    sgu_w_d = nc.dram_tensor("sgu_w", [L, 8, 128, 128], F32, kind="ExternalInput").ap()
    sgu_b_d = nc.dram_tensor("sgu_b", [L, 8, 128], F32, kind="ExternalInput").ap()
    w_fi_d = nc.dram_tensor("w_ffn_in", [L, D, 2 * DFF], F32, kind="ExternalInput").ap()
    w_fo_d = nc.dram_tensor("w_ffn_out", [L, DFF, D], F32, kind="ExternalInput").ap()
    w_ple_d = nc.dram_tensor("w_ple", [L, 256, D], F32, kind="ExternalInput").ap()
    w_pg_d = nc.dram_tensor("w_ple_gate", [L, D, D], F32, kind="ExternalInput").ap()
    out_d = nc.dram_tensor("out", [S, D], F32, kind="ExternalOutput").ap()

    es = contextlib.ExitStack()
    with es:
        P = Prog(nc, es)

        def sb(name, shape, dt):
            return es.enter_context(nc.sbuf_tensor(name, shape, dt))

        PS = es.enter_context(nc.psum_tensor("ps", [128, 8, 512], F32))
        PB = [Buf("bank%d" % i) for i in range(8)]

        X = sb("X", [128, NT, D], F32)
        Xb = [Buf("X%d" % i) for i in range(NT)]
        cosT = sb("cosT", [128, NT, 8], F32)
        sinT = sb("sinT", [128, NT, 8], F32)
        b_trig = Buf("trig")
        ident = sb("ident", [128, 128], BF16)
        identf = sb("identf", [128, 128], F32)
        tri = sb("tri", [128, 128], BF16)
        E8 = sb("E8", [8, 512], BF16)
        b_const = Buf("const")
        ring = [sb("ring%d" % i, [128, 4096], BF16) for i in range(RING)]
        ringb = [[Buf("ring%d_%d" % (i, j)) for j in range(2)] for i in range(RING)]
        rings = [[P.dmasem() for j in range(2)] for i in range(RING)]
        ring_pos = [0]
        lng = sb("lng", [128, D], F32)
        lnb = sb("lnb", [128, D], F32)
        b_lng, b_lnb = Buf("lng"), Buf("lnb")
        s_lng, s_lnb = P.dmasem(), P.dmasem()
        sgg = sb("sgg", [128, 512], F32)
        sgb = sb("sgb", [128, 512], F32)
        b_sgg, b_sgb = Buf("sgg"), Buf("sgb")
        s_sgg, s_sgb = P.dmasem(), P.dmasem()
        WsT = sb("WsT", [128, 8, 128], BF16)
        b_WsT = Buf("WsT")
        sgub = sb("sgub", [8, 128], BF16)
        b_sgub = Buf("sgub")
        s_sgub = P.dmasem()
        kT = sb("kT", [128, S], BF16)
        kiT = sb("kiT", [128, S], BF16)
        V = sb("V", [128, NT, 2, 65], BF16)
        b_kT = [Buf("kT%d" % i) for i in range(NT)]
        b_kiT = [Buf("kiT%d" % i) for i in range(NT)]
        b_V = [Buf("V%d" % i) for i in range(NT)]
        xT = sb("xT", [128, 8, 512], BF16)
        b_xT = [Buf("xT%d" % i) for i in range(4)]
        xb = sb("xb", [128, D], BF16)
        b_xb = Buf("xb")
        QK = sb("QK", [128, 4, 16, 64], BF16)
        b_QK = [Buf("QK%d" % i) for i in range(4)]
        qT = sb("qT", [128, 4, 512], BF16)
        b_qT = [Buf("qT%d" % i) for i in range(4)]
        qiT = sb("qiT", [128, 2, 512], BF16)
        b_qiT = [Buf("qiT%d" % i) for i in range(4)]
        wi = sb("wi", [128, 4, 4], F32)
        b_wi = [Buf("wi%d" % i) for i in range(4)]
        bout = sb("bout", [128, 4, 512], BF16)
        b_bout = [Buf("bout%d" % i) for i in range(4)]
        rt = [sb("rt%d" % i, [128, 128], F32) for i in range(4)]
        b_rt = [Buf("rt%d" % i) for i in range(4)]
        arena = sb("arena", [128, 9216], F32)
        TOK = Buf("arena_tok")

        def carve(off, shape, dt):
            n = 1
            for d in shape[1:]:
                n *= d
            nb = n * (4 if dt == F32 else 2)
            ap = arena[:, off // 4:(off + nb) // 4]
            if dt != F32:
                ap = ap.bitcast(dt)
            if len(shape) == 3:
                ap = ap.rearrange("p (a b) -> p a b", a=shape[1])
            return ap
        score = carve(0, [128, S], F32)
        b_score = Buf("score", TOK)
        scoreB = sb("scoreB", [128, S], F32)
        score2 = [score, scoreB]
        b_score2 = [b_score, Buf("scoreB")]
        relu = [carve(8192 + 4096 * i, [128, 4, 512], BF16) for i in range(2)]
        b_relu = [Buf("relu%d" % i, TOK) for i in range(2)]
        mask = [carve(16384 + 4096 * i, [128, S], BF16) for i in range(2)]
        b_mask = [Buf("mask%d" % i, TOK) for i in range(2)]
        diagw = sb("diagw", [128, 4, 128], BF16)
        b_diagw = Buf("diagw")
        bis = sb("bis", [128, 8], F32)
        b_bis = Buf("bis")
        b_bis2, b_bmid, b_bcnt, b_bind = Buf("bis2"), Buf("bmid"), Buf("bcnt"), Buf("bind")
        mx8 = sb("mx8", [128, 8], F32)
        expP = [carve(24576 + 2048 * i, [128, 1024], BF16) for i in range(2)]
        b_expP = [Buf("expP%d" % i, TOK) for i in range(2)]
        Pm = [carve(28672 + 2048 * i, [128, 8, 128], BF16) for i in range(2)]
        b_Pm = [Buf("Pm%d" % i, TOK) for i in range(2)]
        rs = sb("rs", [128, 8], F32)
        b_rs = Buf("rs")
        cat = sb("cat", [128, D], BF16)
        b_cat = Buf("cat")
        catT = sb("catT", [128, 8, 128], BF16)
        b_catT = Buf("catT")
        lnst = sb("lnst", [128, 2, 6], F32)
        lnmv = sb("lnmv", [128, 4], F32)
        b_lnst = Buf("lnst")
        b_lnst2 = Buf("lnst2")
        b_lnmv = Buf("lnmv")
        gu = carve(32768, [128, 512], BF16)
        b_gu = Buf("gu", TOK)
        gv = carve(33792, [128, 512], F32)
        b_gv = Buf("gv", TOK)
        gvn = carve(35840, [128, 512], BF16)
        b_gvn = Buf("gvn", TOK)
        gT = carve(0, [128, NFC, 512], BF16)
        b_gT = [Buf("gT%d" % i, TOK) for i in range(NFC)]
        p_sb = carve(22528, [128, 4, 256], BF16)
        b_psb = Buf("p_sb", TOK)
        s_psb = P.dmasem()
        pT = carve(24576, [128, 2, 512], BF16)
        b_pT = Buf("pT", TOK)
        ffnT = [carve(26624 + 2048 * i, [128, 512], F32) for i in range(2)]
        b_ffnT = [Buf("ffnT%d" % i, TOK) for i in range(2)]
        silu = [carve(30720 + 1024 * i, [128, 512], BF16) for i in range(2)]
        b_silu = [Buf("silu%d" % i, TOK) for i in range(2)]
        sig = carve(32768, [128, D], F32)
        b_sig = Buf("sig", TOK)
        small = sb("small", [128, 16, 8], F32)
        small2 = sb("small2", [128, 16, 8], F32)
        posi = sb("posi", [128, NT], I32)
        posf = sb("posf", [128, NT], F32)
        invf = sb("invf", [128, 8], F32)
        halfpi = sb("halfpi", [128, 1], F32)
        s_pos = P.dmasem()
        b_pos = Buf("pos")
        s_x = [P.dmasem() for _ in range(NG)]
        s_out = [P.dmasem() for _ in range(NG)]

        def bank(i):
            return PS[:, i, :]

        def bankbf(i):
            return PS[:, i, :].bitcast(BF16)

        for g in range(NG):
            P.op("sp", lambda h, g=g: h.dma_start(
                out=X[:, 4 * g:4 * g + 4, :],
                in_=x_d[512 * g:512 * g + 512, :].rearrange("(n p) d -> p n d", p=128)),
                writes=Xb[4 * g:4 * g + 4], dma=s_x[g])
        P.op("sp", lambda h: h.dma_start(out=posi[:], in_=pos_d[:, :]), writes=[b_pos], dma=s_pos)

        def pool1(fn, reads=(), writes=()):
            P.op("pool", fn, reads=list(reads), writes=list(writes))
        bc = {k: Buf("c_" + k) for k in ["ident", "identf", "tri", "E8", "invf", "halfpi"]}
        pool1(lambda h: h.memset(ident[:], 1.0), writes=[bc["ident"]])
        pool1(lambda h: h.affine_select(out=ident[:], in_=ident[:], pattern=[[-1, 128]], compare_op=ALU.is_equal,
                                        fill=0.0, base=0, channel_multiplier=1), reads=[bc["ident"]], writes=[bc["ident"]])
        pool1(lambda h: h.memset(identf[:], 1.0), writes=[bc["identf"]])
        pool1(lambda h: h.affine_select(out=identf[:], in_=identf[:], pattern=[[-1, 128]], compare_op=ALU.is_equal,
                                        fill=0.0, base=0, channel_multiplier=1), reads=[bc["identf"]], writes=[bc["identf"]])
        pool1(lambda h: h.memset(tri[:], 1.0), writes=[bc["tri"]])
        pool1(lambda h: h.affine_select(out=tri[:], in_=tri[:], pattern=[[1, 128]], compare_op=ALU.is_ge,
                                        fill=0.0, base=0, channel_multiplier=-1), reads=[bc["tri"]], writes=[bc["tri"]])
        pool1(lambda h: h.memset(E8[:], 1.0), writes=[bc["E8"]])
        pool1(lambda h: h.affine_select(out=E8[:], in_=E8[:], pattern=[[1, 512]], compare_op=ALU.is_ge,
                                        fill=0.0, base=0, channel_multiplier=-64), reads=[bc["E8"]], writes=[bc["E8"]])
        pool1(lambda h: h.affine_select(out=E8[:], in_=E8[:], pattern=[[-1, 512]], compare_op=ALU.is_ge,
                                        fill=0.0, base=63, channel_multiplier=64), reads=[bc["E8"]], writes=[bc["E8"]])
        pool1(lambda h: h.memset(V[:], 1.0), writes=b_V)
        for i8 in range(8):
            pool1(lambda h, i8=i8: h.memset(invf[:, i8:i8 + 1], float(np.float32(500000.0) ** np.float32(-2.0 * i8 / 16))),
                  writes=[bc["invf"]])
        pool1(lambda h: h.memset(halfpi[:], math.pi / 2), writes=[bc["halfpi"]])
        pool1(lambda h: h.memset(small[:, 0, 0:1], 0.0), reads=list(bc.values()), writes=[b_const])

        twopi = 2 * math.pi
        c1 = 6.28125
        c2 = float(np.float32(twopi - c1))
        c2 = float(np.frombuffer(np.array([np.frombuffer(np.float32(c2).tobytes(), np.uint32)[0] & 0xFFFFF000],
                                          np.uint32).tobytes(), np.float32)[0])
        c3 = float(twopi - c1 - c2)
        MAGIC = 12582912.0

        b_s1, b_s2 = Buf("small"), Buf("small2")

        def dve1(fn, reads=(), writes=()):
            P.op("dve", fn, reads=list(reads), writes=list(writes))
        dve1(lambda h: h.tensor_copy(out=posf[:], in_=posi[:]), reads=[b_pos, b_const], writes=[b_trig])
        dve1(lambda h: h.tensor_tensor(out=small[:], in0=posf[:].unsqueeze(2).to_broadcast([128, NT, 8]),
                                       in1=invf[:].unsqueeze(1).to_broadcast([128, NT, 8]), op=ALU.mult),
             reads=[b_trig, b_const], writes=[b_s1])
        dve1(lambda h: h.tensor_scalar(out=small2[:], in0=small[:], scalar1=float(1.0 / twopi), scalar2=None, op0=ALU.mult),
             reads=[b_s1], writes=[b_s2])
        dve1(lambda h: h.tensor_scalar(out=small2[:], in0=small2[:], scalar1=MAGIC, scalar2=None, op0=ALU.add),
             reads=[b_s2], writes=[b_s2])
        dve1(lambda h: h.tensor_scalar(out=small2[:], in0=small2[:], scalar1=-MAGIC, scalar2=None, op0=ALU.add),
             reads=[b_s2], writes=[b_s2])
        for c in (c1, c2, c3):
            dve1(lambda h, c=c: h.scalar_tensor_tensor(out=small[:], in0=small2[:], scalar=-c, in1=small[:],
                                                       op0=ALU.mult, op1=ALU.add),
                 reads=[b_s1, b_s2], writes=[b_s1])
        dve1(lambda h: h.tensor_scalar(out=small[:], in0=small[:], scalar1=3.1415925, scalar2=-3.1415925,
                                       op0=ALU.min, op1=ALU.max), reads=[b_s1], writes=[b_s1])
        dve1(lambda h: h.tensor_scalar(out=small2[:], in0=small[:], scalar1=-1.0, scalar2=None, op0=ALU.mult),
             reads=[b_s1, b_s2], writes=[b_s2])
        dve1(lambda h: h.tensor_tensor(out=small2[:], in0=small[:], in1=small2[:], op=ALU.max),
             reads=[b_s1, b_s2], writes=[b_s2])
        b_trig2 = Buf("trig2")
        P.op("act", lambda h: h.activation(out=sinT[:], in_=small[:], func=AF.Sin), reads=[b_s1], writes=[b_trig2])
        P.op("act", lambda h: h.activation(out=cosT[:], in_=small2[:], func=AF.Sin, scale=-1.0, bias=halfpi[:]),
             reads=[b_s2, b_const], writes=[b_trig2])

        def ring_next():
            i = ring_pos[0] % RING
            ring_pos[0] += 1
            return i

        def load_w(slot, half, dst_ap, src_ap, split=False):
            P.op("pool", lambda h: h.dma_start(out=dst_ap, in_=src_ap),
                 writes=([ringb[slot][half]] if split else ringb[slot]), dma=rings[slot][half])

        def transposes_to(src_tile_ap_fn, n, bankid, reads):
            for c in range(n):
                P.op("pe", lambda h, c=c: h.transpose(out=bankbf(bankid)[:, c * 128:(c + 1) * 128],
                                                       in_=src_tile_ap_fn(c), identity=ident[:]),
                     reads=reads + [b_const], writes=[PB[bankid]])

        def layer_norm_inplace(i, gbuf, bbuf, g_ap, b_ap):
            xi = X[:, i, :]

            P.op("dve", lambda h: h.bn_stats(out=lnst[:, 0, :], in_=X[:, i, 0:512]), reads=[Xb[i]], writes=[b_lnst])
            P.op("dve", lambda h: h.bn_stats(out=lnst[:, 1, :], in_=X[:, i, 512:1024]), reads=[Xb[i]], writes=[b_lnst2])
            P.op("dve", lambda h: h.bn_aggr(out=lnmv[:, 0:2], in_=lnst[:].rearrange("p a b -> p (a b)")),
                 reads=[b_lnst, b_lnst2], writes=[b_lnmv])
            P.op("dve", lambda h: h.tensor_scalar(out=lnmv[:, 2:3], in0=lnmv[:, 1:2], scalar1=EPS, scalar2=None, op0=ALU.add),
                 reads=[b_lnmv], writes=[b_lnmv])
            P.op("act", lambda h: h.activation(out=lnmv[:, 2:3], in_=lnmv[:, 2:3], func=AF.Sqrt),
                 reads=[b_lnmv], writes=[b_lnmv])
            P.op("dve", lambda h: h.reciprocal(out=lnmv[:, 2:3], in_=lnmv[:, 2:3]), reads=[b_lnmv], writes=[b_lnmv])
            P.op("dve", lambda h: h.tensor_scalar(out=lnmv[:, 3:4], in0=lnmv[:, 0:1], scalar1=lnmv[:, 2:3], scalar2=-1.0,
                                                  op0=ALU.mult, op1=ALU.mult), reads=[b_lnmv], writes=[b_lnmv])
            P.op("act", lambda h: h.activation(out=xi, in_=xi, func=AF.Identity, scale=lnmv[:, 2:3], bias=lnmv[:, 3:4]),
                 reads=[b_lnmv, Xb[i]], writes=[Xb[i]])
            P.op("dve", lambda h: h.tensor_tensor(out=xi, in0=xi, in1=g_ap, op=ALU.mult),
                 reads=[Xb[i], gbuf], writes=[Xb[i]])
            P.op("dve", lambda h: h.tensor_tensor(out=xi, in0=xi, in1=b_ap, op=ALU.add),
                 reads=[Xb[i], bbuf], writes=[Xb[i]])

        def load_vec(dst, buf, sem, name, l):
            P.op("sp", lambda h: h.dma_start(out=dst[:], in_=vec_d[name][l, :].partition_broadcast(128)),
                 writes=[buf], dma=sem)

        def body_layers():
          for l in range(L):
              stage(1)
              load_vec(sgg, b_sgg, s_sgg, "sgu_ln_g", l)
              load_vec(sgb, b_sgb, s_sgb, "sgu_ln_b", l)
              sl = ring_next()
              load_w(sl, 0, ring[sl][:, 0:1024].rearrange("p (g s) -> p g s", g=8),
                     sgu_w_d[l].rearrange("g t s -> t g s"))
              for g8 in range(8):
                  P.op("pe", lambda h, g8=g8, sl=sl: h.transpose(out=bankbf(0)[:, g8 * 128:(g8 + 1) * 128],
                                                                 in_=ring[sl][:, g8 * 128:(g8 + 1) * 128], identity=ident[:]),
                       reads=ringb[sl] + [b_const], writes=[PB[0]])
              P.op("dve", lambda h: h.tensor_tensor(out=WsT[:], in0=bankbf(0).rearrange("p (g t) -> p g t", g=8),
                                                    in1=tri[:].unsqueeze(1).to_broadcast([128, 8, 128]), op=ALU.mult),
                   reads=[PB[0], b_const], writes=[b_WsT])
              P.op("pool", lambda h, l=l: h.dma_start(out=sgub[:], in_=sgu_b_d[l]), writes=[b_sgub], dma=s_sgub)

              for g in range(NG):
                  stage(2 + g * 10)
                  P.op("dve", lambda h: h.memset(mx8[:, 0:1], 0.0), writes=[TOK])
                  for tt in range(4):
                      i = 4 * g + tt
                      P.op("act", lambda h, i=i: h.activation(func=AF.Identity, out=xb[:], in_=X[:, i, :]), reads=[Xb[i]], writes=[b_xb])
                      tb = tt % 2
                      transposes_to(lambda c: xb[:, c * 128:(c + 1) * 128], 8, tb, [b_xb])
                      P.op("dve", lambda h, tt=tt, tb=tb: h.tensor_copy(
                          out=xT[:, :, tt * 128:(tt + 1) * 128], in_=bankbf(tb).rearrange("p (c t) -> p c t", c=8)),
                          reads=[PB[tb]], writes=[b_xT[tt]])
                  stage(2.1 + g * 10)
                  wsl = []
                  for ci in range(3):
                      c0, cw = CH[ci]
                      sl = ring_next()
                      wsl.append(sl)
                      load_w(sl, 0, ring[sl][:, 0:8 * cw].rearrange("p (c n) -> p c n", c=8),
                             w_in_d[l, :, c0:c0 + cw].rearrange("(c p) n -> p c n", p=128))
                  for tt in range(4):
                      i = 4 * g + tt
                      for ci in range(3):
                          c0, cw = CH[ci]
                          sl = wsl[ci]
                          bk = 2 + ci
                          wv = ring[sl][:, 0:8 * cw].rearrange("p (c n) -> p c n", c=8)
                          for kc in range(8):
                              P.op("pe", lambda h, kc=kc, bk=bk, wv=wv, cw=cw, tt=tt: h.matmul(
                                  bank(bk)[:, 0:cw], lhsT=xT[:, kc, tt * 128:(tt + 1) * 128], rhs=wv[:, kc, :],
                                  start=(kc == 0), stop=(kc == 7)),
                                  reads=[b_xT[tt]] + ringb[sl], writes=[PB[bk]])
                      stage(2.2 + g * 10)
                      bA = bank(2)
                      x1 = bA[:, 0:120].rearrange("p (h d) -> p h d", d=8)
                      x2 = bA[:, 120:240].rearrange("p (h d) -> p h d", d=8)
                      cb = cosT[:, i:i + 1, :].to_broadcast([128, 15, 8])
                      sbb = sinT[:, i:i + 1, :].to_broadcast([128, 15, 8])
                      t1 = rt[0][:, 0:120].rearrange("p (h d) -> p h d", d=8)
                      t2 = rt[1][:, 0:120].rearrange("p (h d) -> p h d", d=8)
                      t3 = rt[2][:, 0:120].rearrange("p (h d) -> p h d", d=8)
                      t4 = rt[3][:, 0:120].rearrange("p (h d) -> p h d", d=8)

                      def rope(h, x1=x1, x2=x2, cb=cb, sbb=sbb, tt=tt):
                          h.tensor_tensor(out=t1, in0=x1, in1=cb, op=ALU.mult)
                          h.tensor_tensor(out=t2, in0=x2, in1=sbb, op=ALU.mult)
                          h.tensor_tensor(out=t3, in0=x2, in1=cb, op=ALU.mult)
                          return h.tensor_tensor(out=t4, in0=x1, in1=sbb, op=ALU.mult)
                      P.op("dve", rope, reads=[PB[2], b_trig2], writes=b_rt)

                      def rope2(h, tt=tt):
                          h.tensor_tensor(out=QK[:, tt, 0:15, 0:8], in0=t1, in1=t2, op=ALU.subtract)
                          return h.tensor_tensor(out=QK[:, tt, 0:15, 8:16], in0=t3, in1=t4, op=ALU.add)
                      P.op("dve", rope2, reads=b_rt, writes=[b_QK[tt]])

                      def rope3(h, tt=tt):
                          h.tensor_tensor(out=QK[:, tt, 15, 0:8], in0=rt[0][:, 112:120], in1=rt[1][:, 112:120], op=ALU.subtract)
                          return h.tensor_tensor(out=QK[:, tt, 15, 8:16], in0=rt[2][:, 112:120], in1=rt[3][:, 112:120],
                                                 op=ALU.add)
                      if "b" in KDBG:
                          P.op("dve", rope3, reads=b_rt, writes=[b_QK[tt]])
                      if "c" in KDBG:
                          P.op("act", lambda h, tt=tt, bA=bA: h.activation(func=AF.Identity, out=QK[:, tt, 15, 16:64],
                                                                           in_=bA[:, 436:484]),
                               reads=[PB[2]], writes=[b_QK[tt]])
                      stage(2.3 + g * 10)
                      P.op("dve", lambda h, tt=tt, bA=bA: h.tensor_scalar(out=wi[:, tt, :], in0=bA[:, 240:244], scalar1=1.0 / 16,
                                                                          scalar2=None, op0=ALU.mult),
                           reads=[PB[2]], writes=[b_wi[tt]])
                      stage(2.3 + 0.01 * 1 + g * 10)
                      P.op("act", lambda h, tt=tt, bA=bA: h.activation(func=AF.Identity, out=QK[:, tt, 10:15, 16:64],
                                                                 in_=bA[:, 244:484].rearrange("p (h d) -> p h d", d=48)),
                           reads=[PB[2]], writes=[b_QK[tt]])
                      stage(2.3 + 0.01 * 2 + g * 10)
                      P.op("act", lambda h, tt=tt: h.activation(func=AF.Identity, out=QK[:, tt, 0:10, 16:64],
                                                          in_=bank(3)[:, 0:480].rearrange("p (h d) -> p h d", d=48)),
                           reads=[PB[3]], writes=[b_QK[tt]])
                      stage(2.3 + 0.01 * 3 + g * 10)
                      P.op("act", lambda h, i=i: h.activation(func=AF.Identity, out=V[:, i, :, 0:64],
                                                        in_=bank(4)[:, 0:128].rearrange("p (g d) -> p g d", d=64)),
                           reads=[PB[4]], writes=[b_V[i]])
                      stage(2.3 + 0.01 * 4 + g * 10)
                      stage(2.4 + g * 10)
                      tb = tt % 2
                      qkf = QK[:, tt, :, :].rearrange("p s d -> p (s d)")
                      transposes_to(lambda c, qkf=qkf: qkf[:, c * 128:(c + 1) * 128], 8, tb, [b_QK[tt]])
                      bb = bankbf(tb).rearrange("p (c t) -> p c t", c=8)
                      P.op("dve", lambda h, tt=tt, bb=bb: h.tensor_copy(out=qT[:, :, tt * 128:(tt + 1) * 128], in_=bb[:, 0:4, :]),
                           reads=[PB[tb]], writes=[b_qT[tt]])
                      P.op("dve", lambda h, i=i, bb=bb: h.tensor_copy(out=kT[:, i * 128:(i + 1) * 128], in_=bb[:, 4, :]),
                           reads=[PB[tb]], writes=[b_kT[i]])
                      P.op("dve", lambda h, tt=tt, bb=bb: h.tensor_copy(out=qiT[:, :, tt * 128:(tt + 1) * 128], in_=bb[:, 5:7, :]),
                           reads=[PB[tb]], writes=[b_qiT[tt]])
                      P.op("dve", lambda h, i=i, bb=bb: h.tensor_copy(out=kiT[:, i * 128:(i + 1) * 128], in_=bb[:, 7, :]),
                           reads=[PB[tb]], writes=[b_kiT[i]])

                  stage(3 + g * 10)
                  sD, sE = ring_next(), ring_next()
                  for sl, ci in ((sD, 3), (sE, 4)):
                      c0, cw = CH[ci]
                      load_w(sl, 0, ring[sl][:, 0:8 * cw].rearrange("p (c n) -> p c n", c=8),
                             w_in_d[l, :, c0:c0 + cw].rearrange("(c p) n -> p c n", p=128))
                  for tt in range(4):
                      i = 4 * g + tt
                      for sl, bk in ((sD, 5), (sE, 6)):
                          wv = ring[sl][:].rearrange("p (c n) -> p c n", c=8)
                          for kc in range(8):
                              P.op("pe", lambda h, kc=kc, bk=bk, wv=wv, tt=tt: h.matmul(
                                  bank(bk), lhsT=xT[:, kc, tt * 128:(tt + 1) * 128], rhs=wv[:, kc, :],
                                  start=(kc == 0), stop=(kc == 7)),
                                  reads=[b_xT[tt]] + ringb[sl], writes=[PB[bk]])
                      P.op("act", lambda h: h.activation(out=gu[:], in_=bank(5), func=AF.Gelu_apprx_tanh),
                           reads=[PB[5]], writes=[b_gu])
                      P.op("act", lambda h: h.activation(out=gv[:], in_=bank(6), func=AF.Gelu_apprx_tanh),
                           reads=[PB[6]], writes=[b_gv])

                      P.op("dve", lambda h: h.bn_stats(out=lnst[:, 0, :], in_=gv[:]), reads=[b_gv], writes=[b_lnst])
                      P.op("dve", lambda h: h.bn_aggr(out=lnmv[:, 0:2], in_=lnst[:, 0, :]), reads=[b_lnst], writes=[b_lnmv])
                      P.op("dve", lambda h: h.tensor_scalar(out=lnmv[:, 2:3], in0=lnmv[:, 1:2], scalar1=EPS, scalar2=None,
                                                            op0=ALU.add), reads=[b_lnmv], writes=[b_lnmv])
                      P.op("act", lambda h: h.activation(out=lnmv[:, 2:3], in_=lnmv[:, 2:3], func=AF.Sqrt),
                           reads=[b_lnmv], writes=[b_lnmv])
                      P.op("dve", lambda h: h.reciprocal(out=lnmv[:, 2:3], in_=lnmv[:, 2:3]), reads=[b_lnmv], writes=[b_lnmv])
                      P.op("dve", lambda h: h.tensor_scalar(out=lnmv[:, 3:4], in0=lnmv[:, 0:1], scalar1=lnmv[:, 2:3],
                                                            scalar2=-1.0, op0=ALU.mult, op1=ALU.mult),
                           reads=[b_lnmv], writes=[b_lnmv])
                      P.op("act", lambda h: h.activation(out=gv[:], in_=gv[:], func=AF.Identity, scale=lnmv[:, 2:3],
                                                         bias=lnmv[:, 3:4]),
                           reads=[b_lnmv, b_gv], writes=[b_gv])
                      P.op("dve", lambda h: h.tensor_tensor(out=gv[:], in0=gv[:], in1=sgg[:], op=ALU.mult),
                           reads=[b_gv, b_sgg], writes=[b_gv])
                      P.op("dve", lambda h: h.tensor_tensor(out=gvn[:], in0=gv[:], in1=sgb[:], op=ALU.add),
                           reads=[b_gv, b_sgb], writes=[b_gvn])
                      for g8 in range(8):
                          P.op("pe", lambda h, g8=g8: h.matmul(bank(7)[:, g8 * 64:(g8 + 1) * 64], lhsT=WsT[:, g8, :],
                                                               rhs=gvn[:, g8 * 64:(g8 + 1) * 64],
                                                               start=(g8 == 0), stop=False, skip_group_check=True),
                               reads=[b_WsT, b_gvn], writes=[PB[7]])
                      P.op("pe", lambda h: h.matmul(bank(7), lhsT=sgub[:], rhs=E8[:], start=False, stop=True,
                                                    skip_group_check=True),
                           reads=[b_sgub, b_const], writes=[PB[7]])
                      P.op("dve", lambda h, tt=tt: h.tensor_tensor(out=bout[:, tt, :], in0=gu[:], in1=bank(7), op=ALU.mult),
                           reads=[b_gu, PB[7]], writes=[b_bout[tt]])

                  stage(4 + g * 10)
                  sO = [ring_next(), ring_next()]
                  for hh in range(2):
                      load_w(sO[hh], 0, ring[sO[hh]][:].rearrange("p (c n) -> p c n", c=8),
                             w_o_d[l, :, hh * 512:(hh + 1) * 512].rearrange("(c p) n -> p c n", p=128))
                  load_vec(lng, b_lng, s_lng, "ln1_g", l)
                  load_vec(lnb, b_lnb, s_lnb, "ln1_b", l)

                  def prep(tt):
                      i = 4 * g + tt
                      N = 128 * (i + 1)
                      mk = mask[i % 2]
                      bmk = b_mask[i % 2]
                      sc = score2[i % 2]
                      bsc = b_score2[i % 2]
                      thunks = []
                      if i < 2:
                          P.op("dve", lambda h: h.memset(mk[:, 0:N], 1.0), writes=[bmk])
                          P.op("dve", lambda h: h.memset(mk[0:64, N - 64:N], 0.0), reads=[bmk], writes=[bmk])
                          return thunks

                      def dg(h):
                          for hh in range(4):
                              r = h.tensor_scalar(out=diagw[:, hh, :], in0=ident[:], scalar1=wi[:, tt, hh:hh + 1],
                                                  scalar2=None, op0=ALU.mult)
                          return r
                      P.op("dve", dg, reads=[b_wi[tt], b_const], writes=[b_diagw])
                      ncs = (N + 511) // 512
                      for c in range(ncs):
                          wc = min(512, N - 512 * c)
                          rb = c % 2
                          for hh in range(4):
                              pr = (hh % 2) * 64
                              P.op("pe", lambda h, hh=hh, pr=pr, c=c, wc=wc: h.matmul(
                                  bank(hh)[:, 0:wc], lhsT=qiT[pr:pr + 64, hh // 2, tt * 128:(tt + 1) * 128],
                                  rhs=kiT[pr:pr + 64, c * 512:c * 512 + wc], start=True, stop=True),
                                  reads=[b_qiT[tt]] + b_kiT[4 * c:4 * c + 4], writes=[PB[hh]])
                          P.op("act", lambda h, rb=rb, wc=wc: h.activation(out=relu[rb][:, :, 0:wc], in_=PS[:, 0:4, 0:wc],
                                                                           func=AF.Relu),
                               reads=PB[0:4], writes=[b_relu[rb]])
                          sbk = 4 + (c % 2)
                          for hh in range(4):
                              P.op("pe", lambda h, hh=hh, rb=rb, wc=wc, sbk=sbk: h.matmul(
                                  bank(sbk)[:, 0:wc], lhsT=diagw[:, hh, :], rhs=relu[rb][:, hh, 0:wc],
                                  start=(hh == 0), stop=(hh == 3)),
                                  reads=[b_diagw, b_relu[rb]], writes=[PB[sbk]])
                          P.op("dve", lambda h, c=c, wc=wc, sbk=sbk: h.tensor_copy(out=sc[:, c * 512:c * 512 + wc],
                                                                                   in_=bank(sbk)[:, 0:wc]),
                               reads=[PB[sbk]], writes=[bsc])

                      def B1(fn, reads=(), writes=()):
                          P.op("dve", fn, reads=list(reads), writes=list(writes))

                      def t_init():
                          B1(lambda h: h.tensor_reduce(out=bis[:, 0:1], in_=sc[:, 0:N], axis=AX.X, op=ALU.min),
                             reads=[bsc], writes=[b_bis])
                          B1(lambda h: h.tensor_reduce(out=bis[:, 5:6], in_=sc[:, 0:N], axis=AX.X, op=ALU.max),
                             reads=[bsc], writes=[b_bis2])
                          B1(lambda h: h.memset(sc[0:64, N - 64:N], -BIG), reads=[b_bis, b_bis2], writes=[bsc])
                          B1(lambda h: h.tensor_tensor(out=bis[:, 1:2], in0=bis[:, 5:6], in1=bis[:, 0:1], op=ALU.subtract),
                             reads=[b_bis, b_bis2], writes=[b_bis2])
                          B1(lambda h: h.tensor_scalar(out=bis[:, 1:2], in0=bis[:, 1:2], scalar1=1.0001, scalar2=1e-6,
                                                       op0=ALU.mult, op1=ALU.add), reads=[b_bis2], writes=[b_bis2])
                      thunks.append(t_init)
                      for it in range(1, NBIS + 1):
                          def t_it(f=2.0 ** (-it)):
                              B1(lambda h: h.scalar_tensor_tensor(out=bis[:, 2:3], in0=bis[:, 1:2], scalar=f,
                                                                  in1=bis[:, 0:1], op0=ALU.mult, op1=ALU.add),
                                 reads=[b_bis, b_bis2], writes=[b_bmid])
                              B1(lambda h: h.tensor_scalar(out=mk[:, 0:N], in0=sc[:, 0:N], scalar1=bis[:, 2:3],
                                                           scalar2=None, op0=ALU.is_ge, op1=ALU.add,
                                                           accum_out=bis[:, 3:4]),
                                 reads=[bsc, b_bmid], writes=[b_bcnt, bmk])
                              B1(lambda h: h.tensor_scalar(out=bis[:, 4:5], in0=bis[:, 3:4], scalar1=TOPK - 0.5, scalar2=f,
                                                           op0=ALU.is_ge, op1=ALU.mult),
                                 reads=[b_bcnt], writes=[b_bind])
                              B1(lambda h: h.scalar_tensor_tensor(out=bis[:, 0:1], in0=bis[:, 4:5], scalar=bis[:, 1:2],
                                                                  in1=bis[:, 0:1], op0=ALU.mult, op1=ALU.add),
                                 reads=[b_bind, b_bis2, b_bis], writes=[b_bis])
                          thunks.append(t_it)

                      def t_fin():
                          P.op("dve", lambda h: h.tensor_scalar(out=mk[:, 0:N], in0=sc[:, 0:N],
                                                                scalar1=bis[:, 0:1], scalar2=None, op0=ALU.is_ge),
                               reads=[bsc, b_bis], writes=[bmk])
                      thunks.append(t_fin)
                      return thunks

                  def attn(tt, sO=sO):
                      i = 4 * g + tt
                      mk = mask[i % 2]
                      bmk = b_mask[i % 2]
                      thunks = []
                      for j in range(i + 1):
                          def t_j(j=j):
                              pb0 = 0 if j % 2 == 0 else 2
                              eb = j % 2
                              for jj in range(4):
                                  for grp in range(2):
                                      pr = grp * 64
                                      P.op("pe", lambda h, jj=jj, grp=grp, pr=pr: h.matmul(
                                          bank(pb0 + grp)[:, jj * 128:(jj + 1) * 128],
                                          lhsT=kT[pr:pr + 64, j * 128:(j + 1) * 128],
                                          rhs=qT[pr:pr + 64, jj, tt * 128:(tt + 1) * 128], start=True, stop=True),
                                          reads=[b_kT[j], b_qT[tt]], writes=[PB[pb0 + grp]])
                              P.op("act", lambda h: h.activation(
                                  out=expP[eb][:].rearrange("p (b n) -> p b n", b=2), in_=PS[:, pb0:pb0 + 2, :],
                                  func=AF.Exp, scale=0.125),
                                  reads=[PB[pb0], PB[pb0 + 1]], writes=[b_expP[eb]])
                              P.op("pe", lambda h: h.transpose(out=bankbf(6)[:, 0:128], in_=mk[:, j * 128:(j + 1) * 128],
                                                               identity=ident[:]),
                                   reads=[bmk, b_const], writes=[PB[6]])
                              P.op("dve", lambda h: h.tensor_tensor(
                                  out=Pm[eb][:], in0=expP[eb][:].rearrange("p (a t) -> p a t", a=8),
                                  in1=bankbf(6)[:, 0:128].unsqueeze(1).to_broadcast([128, 8, 128]), op=ALU.mult),
                                  reads=[b_expP[eb], PB[6]], writes=[b_Pm[eb]])
                              for grp in range(2):
                                  for jj in range(4):
                                      P.op("pe", lambda h, grp=grp, jj=jj: h.matmul(
                                          bank(4 + grp)[:, jj * 65:(jj + 1) * 65], lhsT=Pm[eb][:, grp * 4 + jj, :],
                                          rhs=V[:, j, grp, :], start=(j == 0 and jj == 0), stop=(j == i and jj == 3),
                                          skip_group_check=True),
                                          reads=[b_Pm[eb], b_V[j]], writes=[PB[4 + grp]])
                          thunks.append(t_j)

                      def t_final():
                          pv = PS[:, 4:6, 0:260].rearrange("p b (h e) -> p b h e", e=65)
                          P.op("dve", lambda h: h.reciprocal(out=rs[:].rearrange("p (b h) -> p b h", b=2), in_=pv[:, :, :, 64]),
                               reads=[PB[4], PB[5]], writes=[b_rs])
                          for grp in range(2):
                              P.op("dve", lambda h, grp=grp: h.tensor_tensor(
                                  out=cat[:, grp * 256:(grp + 1) * 256].rearrange("p (h d) -> p h d", d=64),
                                  in0=pv[:, grp, :, 0:64],
                                  in1=rs[:, grp * 4:(grp + 1) * 4].unsqueeze(2).to_broadcast([128, 4, 64]), op=ALU.mult),
                                  reads=[PB[4 + grp], b_rs], writes=[b_cat])
                          P.op("act", lambda h: h.activation(func=AF.Identity, out=cat[:, 512:1024], in_=bout[:, tt, :]),
                               reads=[b_bout[tt]], writes=[b_cat])
                          transposes_to(lambda c: cat[:, c * 128:(c + 1) * 128], 8, 7, [b_cat])
                          P.op("dve", lambda h: h.tensor_copy(out=catT[:], in_=bankbf(7).rearrange("p (c t) -> p c t", c=8)),
                               reads=[PB[7]], writes=[b_catT])
                          for hh in range(2):
                              wv = ring[sO[hh]][:].rearrange("p (c n) -> p c n", c=8)
                              for kc in range(8):
                                  P.op("pe", lambda h, hh=hh, kc=kc, wv=wv: h.matmul(
                                      bank(hh), lhsT=catT[:, kc, :], rhs=wv[:, kc, :], start=(kc == 0), stop=(kc == 7)),
                                      reads=[b_catT] + ringb[sO[hh]], writes=[PB[hh]])
                          P.op("dve", lambda h: h.scalar_tensor_tensor(
                              out=X[:, i, :].rearrange("p (b n) -> p b n", b=2), in0=X[:, i, :].rearrange("p (b n) -> p b n", b=2),
                              scalar=ALPHA, in1=PS[:, 0:2, :], op0=ALU.mult, op1=ALU.add),
                              reads=[Xb[i], PB[0], PB[1]], writes=[Xb[i]])
                          layer_norm_inplace(i, b_lng, b_lnb, lng[:], lnb[:])
                      return thunks, t_final

                  for t_ in prep(0):
                      t_()
                  for tt in range(4):
                      nxt = prep(tt + 1) if tt < 3 else []
                      A_, fin_ = attn(tt)
                      na, nb, bi = len(A_), len(nxt), 0
                      for k_, a_ in enumerate(A_):
                          a_()
                          tgt = (nb * (k_ + 1)) // na
                          while bi < tgt:
                              nxt[bi]()
                              bi += 1
                      fin_()
                      while bi < nb:
                          nxt[bi]()
                          bi += 1

                  stage(6 + g * 10)
                  P.op("dve", lambda h: h.memset(mx8[:, 0:1], 0.0), writes=[TOK])
                  for tt in range(4):
                      i = 4 * g + tt
                      P.op("act", lambda h, i=i: h.activation(func=AF.Identity, out=xb[:], in_=X[:, i, :]), reads=[Xb[i]], writes=[b_xb])
                      tb = tt % 2
                      transposes_to(lambda c: xb[:, c * 128:(c + 1) * 128], 8, tb, [b_xb])
                      P.op("dve", lambda h, tt=tt, tb=tb: h.tensor_copy(
                          out=xT[:, :, tt * 128:(tt + 1) * 128], in_=bankbf(tb).rearrange("p (c t) -> p c t", c=8)),
                          reads=[PB[tb]], writes=[b_xT[tt]])
                  P.op("pool", lambda h, l=l, g=g: h.dma_start(
                      out=p_sb[:], in_=p_d[l, 512 * g:512 * g + 512, :].rearrange("(n p) d -> p n d", p=128)),
                      writes=[b_psb], dma=s_psb)
                  for tt in range(4):
                      for c in range(2):
                          P.op("pe", lambda h, tt=tt, c=c: h.transpose(
                              out=bankbf(2)[:, (tt * 2 + c) * 128:(tt * 2 + c + 1) * 128],
                              in_=p_sb[:, tt, c * 128:(c + 1) * 128], identity=ident[:]),
                              reads=[b_psb, b_const], writes=[PB[2]])
                  P.op("dve", lambda h: h.tensor_copy(out=pT[:].rearrange("p c (t n) -> p t c n", t=4),
                                               in_=bankbf(2).rearrange("p (t c n) -> p t c n", t=4, c=2)),
                       reads=[PB[2]], writes=[b_pT])
                  for fc in range(NFC):
                      sl = ring_next()
                      wv = ring[sl][:, 0:2048].rearrange("p (c n) -> p c n", c=8)
                      load_w(sl, 0, wv[:, :, 0:128], w_fi_d[l, :, fc * 128:(fc + 1) * 128].rearrange("(c p) n -> p c n", p=128),
                             split=True)
                      load_w(sl, 1, wv[:, :, 128:256],
                             w_fi_d[l, :, DFF + fc * 128:DFF + (fc + 1) * 128].rearrange("(c p) n -> p c n", p=128), split=True)
                      gb = 3 + 2 * (fc % 2)
                      for hf in range(2):
                          for kc in range(8):
                              P.op("pe", lambda h, hf=hf, kc=kc, wv=wv, gb=gb: h.matmul(
                                  bank(gb + hf), lhsT=wv[:, kc, hf * 128:(hf + 1) * 128], rhs=xT[:, kc, :],
                                  start=(kc == 0), stop=(kc == 7)),
                                  reads=b_xT + [ringb[sl][hf]], writes=[PB[gb + hf]])
                      sb_ = fc % 2
                      P.op("act", lambda h, gb=gb, sb_=sb_: h.activation(out=silu[sb_][:], in_=bank(gb), func=AF.Silu),
                           reads=[PB[gb]], writes=[b_silu[sb_]])
                      P.op("dve", lambda h, gb=gb, sb_=sb_, fc=fc: h.tensor_tensor(out=gT[:, fc, :], in0=silu[sb_][:],
                                                                                   in1=bank(gb + 1), op=ALU.mult),
                           reads=[b_silu[sb_], PB[gb + 1]], writes=[b_gT[fc]])
                  for n8 in range(8):
                      sl = ring_next()
                      wv = ring[sl][:, 0:NFC * 128].rearrange("p (c n) -> p c n", c=NFC)
                      load_w(sl, 0, wv, w_fo_d[l, :, n8 * 128:(n8 + 1) * 128].rearrange("(c p) n -> p c n", p=128))
                      ob = n8 % 2
                      for fc in range(NFC):
                          P.op("pe", lambda h, fc=fc, wv=wv, ob=ob: h.matmul(
                              bank(ob), lhsT=wv[:, fc, :], rhs=gT[:, fc, :], start=(fc == 0), stop=(fc == NFC - 1)),
                              reads=[b_gT[fc]] + ringb[sl], writes=[PB[ob]])
                      P.op("act", lambda h, ob=ob: h.activation(func=AF.Identity, out=ffnT[ob][:], in_=bank(ob)), reads=[PB[ob]], writes=[b_ffnT[ob]])
                      for tt in range(4):
                          P.op("pe", lambda h, tt=tt, ob=ob: h.transpose(out=bank(2)[:, tt * 128:(tt + 1) * 128],
                                                                         in_=ffnT[ob][:, tt * 128:(tt + 1) * 128],
                                                                         identity=identf[:]),
                               reads=[b_ffnT[ob], b_const], writes=[PB[2]])
                      P.op("dve", lambda h, n8=n8, g=g: h.scalar_tensor_tensor(
                          out=X[:, 4 * g:4 * g + 4, n8 * 128:(n8 + 1) * 128], in0=X[:, 4 * g:4 * g + 4, n8 * 128:(n8 + 1) * 128],
                          scalar=ALPHA, in1=bank(2).rearrange("p (t n) -> p t n", t=4), op0=ALU.mult, op1=ALU.add),
                          reads=[PB[2]] + Xb[4 * g:4 * g + 4], writes=Xb[4 * g:4 * g + 4])
                  stage(7 + g * 10)
                  sP = ring_next()
                  wpl = ring[sP][:, 0:2048].rearrange("p (c n) -> p c n", c=2)
                  load_w(sP, 0, wpl, w_ple_d[l].rearrange("(c p) n -> p c n", p=128))
                  sG = [ring_next(), ring_next()]
                  for hh in range(2):
                      load_w(sG[hh], 0, ring[sG[hh]][:].rearrange("p (c n) -> p c n", c=8),
                             w_pg_d[l, :, hh * 512:(hh + 1) * 512].rearrange("(c p) n -> p c n", p=128))
                  load_vec(lng, b_lng, s_lng, "ln2_g", l)
                  load_vec(lnb, b_lnb, s_lnb, "ln2_b", l)
                  for tt in range(4):
                      i = 4 * g + tt
                      for hh in range(2):
                          for c in range(2):
                              P.op("pe", lambda h, hh=hh, c=c, tt=tt, wpl=wpl: h.matmul(
                                  bank(4 + hh), lhsT=pT[:, c, tt * 128:(tt + 1) * 128], rhs=wpl[:, c, hh * 512:(hh + 1) * 512],
                                  start=(c == 0), stop=(c == 1)),
                                  reads=[b_pT] + ringb[sP], writes=[PB[4 + hh]])
                          wv = ring[sG[hh]][:].rearrange("p (c n) -> p c n", c=8)
                          for kc in range(8):
                              P.op("pe", lambda h, hh=hh, kc=kc, wv=wv, tt=tt: h.matmul(
                                  bank(6 + hh), lhsT=xT[:, kc, tt * 128:(tt + 1) * 128], rhs=wv[:, kc, :],
                                  start=(kc == 0), stop=(kc == 7)),
                                  reads=[b_xT[tt]] + ringb[sG[hh]], writes=[PB[6 + hh]])
                      P.op("act", lambda h: h.activation(out=sig[:].rearrange("p (b n) -> p b n", b=2), in_=PS[:, 6:8, :],
                                                         func=AF.Sigmoid),
                           reads=[PB[6], PB[7]], writes=[b_sig])
                      P.op("dve", lambda h: h.tensor_tensor(out=sig[:].rearrange("p (b n) -> p b n", b=2),
                                                            in0=sig[:].rearrange("p (b n) -> p b n", b=2),
                                                            in1=PS[:, 4:6, :], op=ALU.mult),
                           reads=[b_sig, PB[4], PB[5]], writes=[b_sig])
                      P.op("dve", lambda h, i=i: h.tensor_tensor(out=X[:, i, :], in0=X[:, i, :], in1=sig[:], op=ALU.add),
                           reads=[b_sig, Xb[i]], writes=[Xb[i]])
                      layer_norm_inplace(i, b_lng, b_lnb, lng[:], lnb[:])

        try:
            body_layers()
        except _Stop:
            pass
        for g in range(NG):
            P.op("sp", lambda h, g=g: h.dma_start(
                out=out_d[512 * g:512 * g + 512, :].rearrange("(n p) d -> p n d", p=128),
                in_=X[:, 4 * g:4 * g + 4, :]),
                reads=Xb[4 * g:4 * g + 4], dma=s_out[g])
        fin = Buf("fin")
        for g in range(NG):
            pass
        last = [len(P.ops) - NG + g for g in range(NG)]
        idx = P.op("sp", lambda h: None)
        P.ops[idx][2].update(last)
        P.emit()
    return nc


def prep_inputs(inputs, n_layers, b):
    perm = w_in_perm()
    L = n_layers
    m = {}
    m["x"] = np.ascontiguousarray(inputs["x"][b], dtype=np.float32)
    m["p"] = np.ascontiguousarray(inputs["p"][:L, b], dtype=np.float32)
    m["pos"] = np.ascontiguousarray(np.asarray(inputs["positions"][b]).reshape(NT, 128).T.astype(np.int32))
    return m


def shared_inputs(inputs, n_layers, l0=0):
    perm = w_in_perm()
    L = n_layers
    sl = slice(l0, l0 + L)
    m = {}
    m["w_in"] = np.ascontiguousarray(np.asarray(inputs["w_in"])[sl][:, :, perm], dtype=np.float32)
    for k in ["w_o", "ln1_g", "ln1_b", "ln2_g", "ln2_b", "sgu_ln_g", "sgu_ln_b", "sgu_w", "sgu_b",
              "w_ffn_in", "w_ffn_out", "w_ple", "w_ple_gate"]:
        m[k] = np.ascontiguousarray(np.asarray(inputs[k])[sl], dtype=np.float32)
    return m


_NC_CACHE = {}


def kernel(**inputs):
    inputs = {k: np.asarray(v) for k, v in inputs.items()}
    L = 4
    if L not in _NC_CACHE:
        _NC_CACHE[L] = build(L)
    nc = _NC_CACHE[L]
    sh = shared_inputs(inputs, L)
    in_maps = []
    for b in range(8):
        m = prep_inputs(inputs, L, b)
        m.update(sh)
        in_maps.append(m)
    res = run_bass_kernel_spmd(nc, in_maps, core_ids=list(range(8)))
    out = np.stack([np.asarray(r["out"], dtype=np.float32) for r in res.results], axis=0)
    return out
```

```python
import math
import numpy as np
import concourse.bass as bass
import concourse.mybir as mybir
from concourse.bass_utils import run_bass_kernel_spmd

F32 = mybir.dt.float32
BF16 = mybir.dt.bfloat16
I32 = mybir.dt.int32
AF = mybir.ActivationFunctionType
ALU = mybir.AluOpType
AX = mybir.AxisListType

S = 2048
D = 1024
NT = 16
NG = 4
DFF = 2816
NFC = 22
INC = 2116
ALPHA = 8 ** 0.25
EPS = 1e-5
TOPK = 256
NBIS = 22
BIG = 1.0e30
SAME_ENG_SYNC = True
RING = 4
import os
KDBG = os.environ.get('KDBG', 'bc')

CH = [(0, 484), (484, 480), (964, 128), (1092, 512), (1604, 512)]


def _slot_cols():
    q_order = [0, 4, 1, 5, 2, 6, 3, 7]
    bases = [h * 64 for h in q_order] + [512, 576] + [768 + 64 * h for h in range(4)] + [1024]
    return bases


def w_in_perm():
    bases = _slot_cols()
    cols = []
    for b in bases:
        cols += [b + d for d in range(8)]
    for b in bases:
        cols += [b + 8 + d for d in range(8)]
    cols += [1088 + h for h in range(4)]
    for s in range(10, 15):
        cols += [bases[s] + d for d in range(16, 64)]
    for s in range(0, 10):
        cols += [bases[s] + d for d in range(16, 64)]
    cols += list(range(640, 768))
    cols += list(range(1092, 2116))
    assert len(cols) == INC and len(set(cols)) == INC
    return np.array(cols)


class Buf:
    __slots__ = ("name", "lastw", "readers", "tok")

    def __init__(self, name, tok=None):
        self.name = name
        self.lastw = None
        self.readers = []
        self.tok = tok


class DmaSem:
    def __init__(self, sem):
        self.sem = sem
        self.count = 0


class Prog:
    ENGS = ["pe", "act", "dve", "pool", "sp"]

    def __init__(self, nc, es):
        self.nc = nc
        self.es = es
        self.ops = []
        self.sems = {e: es.enter_context(nc.semaphore("sem_" + e)) for e in self.ENGS}
        self.ndma = 0

    def dmasem(self):
        self.ndma += 1
        return DmaSem(self.es.enter_context(self.nc.semaphore("dsem%d" % self.ndma)))

    def op(self, eng, fn, reads=(), writes=(), dma=None):
        idx = len(self.ops)
        deps = set()
        toks = set(b.tok for b in list(reads) + list(writes) if b.tok is not None)
        if toks:
            reads = list(reads) + [t for t in toks if t not in writes]
        for b in reads:
            if b.lastw is not None:
                deps.add(b.lastw)
        for b in writes:
            if b.lastw is not None:
                deps.add(b.lastw)
            deps.update(b.readers)
        for b in reads:
            b.readers.append(idx)
        for b in writes:
            b.lastw = idx
            b.readers = []
        deps.discard(idx)
        self.ops.append([eng, fn, deps, dma, False, None])
        return idx

    def emit(self):
        nc = self.nc
        ops = self.ops
        for o in ops:
            eng, fn, deps, dma, _, _ = o
            for d in deps:
                p = ops[d]
                if p[3] is None and dma is None and p[0] == eng and (eng == "pe" or not SAME_ENG_SYNC):
                    continue
                p[4] = True
        cnt = {e: 0 for e in self.ENGS}
        for o in ops:
            if o[3] is not None:
                o[3].count += 16
                o[5] = (o[3].sem, o[3].count)
            elif o[4]:
                cnt[o[0]] += 1
                o[5] = (self.sems[o[0]], cnt[o[0]])
        per = {e: [] for e in self.ENGS}
        for o in ops:
            per[o[0]].append(o)
        names = {"pe": "tensor", "act": "scalar", "dve": "vector", "pool": "gpsimd", "sp": "sync"}
        with nc.Block() as block:
            for e in self.ENGS:
                def body(h, e=e):
                    waited = {}
                    for o in per[e]:
                        for d in sorted(o[2]):
                            p = ops[d]
                            if p[5] is None:
                                continue
                            if p[3] is None and o[3] is None and p[0] == e and (e == "pe" or not SAME_ENG_SYNC):
                                continue
                            sem, val = p[5]
                            key = id(sem)
                            if waited.get(key, 0) >= val:
                                continue
                            h.wait_ge(sem, val)
                            waited[key] = val
                        ins = o[1](h)
                        if o[5] is not None and ins is not None:
                            ins.then_inc(o[5][0], 16 if o[3] is not None else 1)
                getattr(block, names[e])(body)


class _Stop(Exception):
    pass


def build(n_layers, dbg=False, stage_limit=10 ** 9):
    import contextlib

    def stage(n):
        if n > stage_limit:
            raise _Stop()
    nc = bass.Bass("TRN2", target_bir_lowering=False)
    L = n_layers
    x_d = nc.dram_tensor("x", [S, D], F32, kind="ExternalInput").ap()
    p_d = nc.dram_tensor("p", [L, S, 256], F32, kind="ExternalInput").ap()
    pos_d = nc.dram_tensor("pos", [128, NT], I32, kind="ExternalInput").ap()
    w_in_d = nc.dram_tensor("w_in", [L, D, INC], F32, kind="ExternalInput").ap()
    w_o_d = nc.dram_tensor("w_o", [L, D, D], F32, kind="ExternalInput").ap()
    vec_d = {}
    for nm, n in [("ln1_g", D), ("ln1_b", D), ("ln2_g", D), ("ln2_b", D), ("sgu_ln_g", 512), ("sgu_ln_b", 512)]:
        vec_d[nm] = nc.dram_tensor(nm, [L, n], F32, kind="ExternalInput").ap()
    sgu_w_d = nc.dram_tensor("sgu_w", [L, 8, 128, 128], F32, kind="ExternalInput").ap()
    sgu_b_d = nc.dram_tensor("sgu_b", [L, 8, 128], F32, kind="ExternalInput").ap()
    w_fi_d = nc.dram_tensor("w_ffn_in", [L, D, 2 * DFF], F32, kind="ExternalInput").ap()
    w_fo_d = nc.dram_tensor("w_ffn_out", [L, DFF, D], F32, kind="ExternalInput").ap()
    w_ple_d = nc.dram_tensor("w_ple", [L, 256, D], F32, kind="ExternalInput").ap()
    w_pg_d = nc.dram_tensor("w_ple_gate", [L, D, D], F32, kind="ExternalInput").ap()
    out_d = nc.dram_tensor("out", [S, D], F32, kind="ExternalOutput").ap()

    es = contextlib.ExitStack()
    with es:
        P = Prog(nc, es)

        def sb(name, shape, dt):
            return es.enter_context(nc.sbuf_tensor(name, shape, dt))

        PS = es.enter_context(nc.psum_tensor("ps", [128, 8, 512], F32))
        PB = [Buf("bank%d" % i) for i in range(8)]

        X = sb("X", [128, NT, D], F32)
        Xb = [Buf("X%d" % i) for i in range(NT)]
        cosT = sb("cosT", [128, NT, 8], F32)
        sinT = sb("sinT", [128, NT, 8], F32)
        b_trig = Buf("trig")
        ident = sb("ident", [128, 128], BF16)
        identf = sb("identf", [128, 128], F32)
        tri = sb("tri", [128, 128], BF16)
        E8 = sb("E8", [8, 512], BF16)
        b_const = Buf("const")
        ring = [sb("ring%d" % i, [128, 4096], BF16) for i in range(RING)]
        ringb = [[Buf("ring%d_%d" % (i, j)) for j in range(2)] for i in range(RING)]
        rings = [[P.dmasem() for j in range(2)] for i in range(RING)]
        ring_pos = [0]
        lng = sb("lng", [128, D], F32)
        lnb = sb("lnb", [128, D], F32)
        b_lng, b_lnb = Buf("lng"), Buf("lnb")
        s_lng, s_lnb = P.dmasem(), P.dmasem()
        sgg = sb("sgg", [128, 512], F32)
        sgb = sb("sgb", [128, 512], F32)
        b_sgg, b_sgb = Buf("sgg"), Buf("sgb")
        s_sgg, s_sgb = P.dmasem(), P.dmasem()
        WsT = sb("WsT", [128, 8, 128], BF16)
        b_WsT = Buf("WsT")
        sgub = sb("sgub", [8, 128], BF16)
        b_sgub = Buf("sgub")
        s_sgub = P.dmasem()
        kT = sb("kT", [128, S], BF16)
        kiT = sb("kiT", [128, S], BF16)
        V = sb("V", [128, NT, 2, 65], BF16)
        b_kT = [Buf("kT%d" % i) for i in range(NT)]
        b_kiT = [Buf("kiT%d" % i) for i in range(NT)]
        b_V = [Buf("V%d" % i) for i in range(NT)]
        xT = sb("xT", [128, 8, 512], BF16)
        b_xT = [Buf("xT%d" % i) for i in range(4)]
        xb = sb("xb", [128, D], BF16)
        b_xb = Buf("xb")
        QK = sb("QK", [128, 4, 16, 64], BF16)
        b_QK = [Buf("QK%d" % i) for i in range(4)]
        qT = sb("qT", [128, 4, 512], BF16)
        b_qT = [Buf("qT%d" % i) for i in range(4)]
        qiT = sb("qiT", [128, 2, 512], BF16)
        b_qiT = [Buf("qiT%d" % i) for i in range(4)]
        wi = sb("wi", [128, 4, 4], F32)
        b_wi = [Buf("wi%d" % i) for i in range(4)]
        bout = sb("bout", [128, 4, 512], BF16)
        b_bout = [Buf("bout%d" % i) for i in range(4)]
        rt = [sb("rt%d" % i, [128, 128], F32) for i in range(4)]
        b_rt = [Buf("rt%d" % i) for i in range(4)]
        arena = sb("arena", [128, 9216], F32)
        TOK = Buf("arena_tok")

        def carve(off, shape, dt):
            n = 1
            for d in shape[1:]:
                n *= d
            nb = n * (4 if dt == F32 else 2)
            ap = arena[:, off // 4:(off + nb) // 4]
            if dt != F32:
                ap = ap.bitcast(dt)
            if len(shape) == 3:
                ap = ap.rearrange("p (a b) -> p a b", a=shape[1])
            return ap
        score = carve(0, [128, S], F32)
        b_score = Buf("score", TOK)
        scoreB = sb("scoreB", [128, S], F32)
        score2 = [score, scoreB]
        b_score2 = [b_score, Buf("scoreB")]
        relu = [carve(8192 + 4096 * i, [128, 4, 512], BF16) for i in range(2)]
        b_relu = [Buf("relu%d" % i, TOK) for i in range(2)]
        mask = [carve(16384 + 4096 * i, [128, S], BF16) for i in range(2)]
        b_mask = [Buf("mask%d" % i, TOK) for i in range(2)]
        diagw = sb("diagw", [128, 4, 128], BF16)
        b_diagw = Buf("diagw")
        bis = sb("bis", [128, 8], F32)
        b_bis = Buf("bis")
        b_bis2, b_bmid, b_bcnt, b_bind = Buf("bis2"), Buf("bmid"), Buf("bcnt"), Buf("bind")
        mx8 = sb("mx8", [128, 8], F32)
        b2 = sb("b2", [128, 80], F32)
        b_NW, b_ss, b_sg, b_cN, b_a, b_nthr = (Buf("NW"), Buf("ss"), Buf("sg"), Buf("cN"), Buf("a"), Buf("nthr"))
        b_nm = [Buf("nm0"), Buf("nm1")]
        expP = [carve(24576 + 2048 * i, [128, 1024], BF16) for i in range(2)]
        b_expP = [Buf("expP%d" % i, TOK) for i in range(2)]
        Pm = [carve(28672 + 2048 * i, [128, 8, 128], BF16) for i in range(2)]
        b_Pm = [Buf("Pm%d" % i, TOK) for i in range(2)]
        rs = sb("rs", [128, 8], F32)
        b_rs = Buf("rs")
        cat = sb("cat", [128, D], BF16)
        b_cat = Buf("cat")
        catT = sb("catT", [128, 8, 128], BF16)
        b_catT = Buf("catT")
        lnst = sb("lnst", [128, 2, 6], F32)
        lnmv = sb("lnmv", [128, 4], F32)
        b_lnst = Buf("lnst")
        b_lnst2 = Buf("lnst2")
        b_lnmv = Buf("lnmv")
        gu = carve(32768, [128, 512], BF16)
        b_gu = Buf("gu", TOK)
        gv = carve(33792, [128, 512], F32)
        b_gv = Buf("gv", TOK)
        gvn = carve(35840, [128, 512], BF16)
        b_gvn = Buf("gvn", TOK)
        gT = carve(0, [128, NFC, 512], BF16)
        b_gT = [Buf("gT%d" % i, TOK) for i in range(NFC)]
        p_sb = carve(22528, [128, 4, 256], BF16)
        b_psb = Buf("p_sb", TOK)
        s_psb = P.dmasem()
        pT = carve(24576, [128, 2, 512], BF16)
        b_pT = Buf("pT", TOK)
        ffnT = [carve(26624 + 2048 * i, [128, 512], F32) for i in range(2)]
        b_ffnT = [Buf("ffnT%d" % i, TOK) for i in range(2)]
        silu = [carve(30720 + 1024 * i, [128, 512], BF16) for i in range(2)]
        b_silu = [Buf("silu%d" % i, TOK) for i in range(2)]
        sig = carve(32768, [128, D], F32)
        b_sig = Buf("sig", TOK)
        small = sb("small", [128, 16, 8], F32)
        small2 = sb("small2", [128, 16, 8], F32)
        posi = sb("posi", [128, NT], I32)
        posf = sb("posf", [128, NT], F32)
        invf = sb("invf", [128, 8], F32)
        halfpi = sb("halfpi", [128, 1], F32)
        s_pos = P.dmasem()
        b_pos = Buf("pos")
        s_x = [P.dmasem() for _ in range(NG)]
        s_out = [P.dmasem() for _ in range(NG)]

        def bank(i):
            return PS[:, i, :]

        def bankbf(i):
            return PS[:, i, :].bitcast(BF16)

        for g in range(NG):
            P.op("sp", lambda h, g=g: h.dma_start(
                out=X[:, 4 * g:4 * g + 4, :],
                in_=x_d[512 * g:512 * g + 512, :].rearrange("(n p) d -> p n d", p=128)),
                writes=Xb[4 * g:4 * g + 4], dma=s_x[g])
        P.op("sp", lambda h: h.dma_start(out=posi[:], in_=pos_d[:, :]), writes=[b_pos], dma=s_pos)

        def pool1(fn, reads=(), writes=()):
            P.op("pool", fn, reads=list(reads), writes=list(writes))
        bc = {k: Buf("c_" + k) for k in ["ident", "identf", "tri", "E8", "invf", "halfpi"]}
        pool1(lambda h: h.memset(ident[:], 1.0), writes=[bc["ident"]])
        pool1(lambda h: h.affine_select(out=ident[:], in_=ident[:], pattern=[[-1, 128]], compare_op=ALU.is_equal,
                                        fill=0.0, base=0, channel_multiplier=1), reads=[bc["ident"]], writes=[bc["ident"]])
        pool1(lambda h: h.memset(identf[:], 1.0), writes=[bc["identf"]])
        pool1(lambda h: h.affine_select(out=identf[:], in_=identf[:], pattern=[[-1, 128]], compare_op=ALU.is_equal,
                                        fill=0.0, base=0, channel_multiplier=1), reads=[bc["identf"]], writes=[bc["identf"]])
        pool1(lambda h: h.memset(tri[:], 1.0), writes=[bc["tri"]])
        pool1(lambda h: h.affine_select(out=tri[:], in_=tri[:], pattern=[[1, 128]], compare_op=ALU.is_ge,
                                        fill=0.0, base=0, channel_multiplier=-1), reads=[bc["tri"]], writes=[bc["tri"]])
        pool1(lambda h: h.memset(E8[:], 1.0), writes=[bc["E8"]])
        pool1(lambda h: h.affine_select(out=E8[:], in_=E8[:], pattern=[[1, 512]], compare_op=ALU.is_ge,
                                        fill=0.0, base=0, channel_multiplier=-64), reads=[bc["E8"]], writes=[bc["E8"]])
        pool1(lambda h: h.affine_select(out=E8[:], in_=E8[:], pattern=[[-1, 512]], compare_op=ALU.is_ge,
                                        fill=0.0, base=63, channel_multiplier=64), reads=[bc["E8"]], writes=[bc["E8"]])
        pool1(lambda h: h.memset(V[:], 1.0), writes=b_V)
        for i8 in range(8):
            pool1(lambda h, i8=i8: h.memset(invf[:, i8:i8 + 1], float(np.float32(500000.0) ** np.float32(-2.0 * i8 / 16))),
                  writes=[bc["invf"]])
        pool1(lambda h: h.memset(halfpi[:], math.pi / 2), writes=[bc["halfpi"]])
        bc["fk"] = Buf("c_fk")
        for k in range(NBIS + 1):
            pool1(lambda h, k=k: h.memset(b2[:, k:k + 1], -(2.0 ** (-(k + 1)))), writes=[bc["fk"]])
        pool1(lambda h: h.memset(small[:, 0, 0:1], 0.0), reads=list(bc.values()), writes=[b_const])

        twopi = 2 * math.pi
        c1 = 6.28125
        c2 = float(np.float32(twopi - c1))
        c2 = float(np.frombuffer(np.array([np.frombuffer(np.float32(c2).tobytes(), np.uint32)[0] & 0xFFFFF000],
                                          np.uint32).tobytes(), np.float32)[0])
        c3 = float(twopi - c1 - c2)
        MAGIC = 12582912.0

        b_s1, b_s2 = Buf("small"), Buf("small2")

        def dve1(fn, reads=(), writes=()):
            P.op("dve", fn, reads=list(reads), writes=list(writes))
        dve1(lambda h: h.tensor_copy(out=posf[:], in_=posi[:]), reads=[b_pos, b_const], writes=[b_trig])
        dve1(lambda h: h.tensor_tensor(out=small[:], in0=posf[:].unsqueeze(2).to_broadcast([128, NT, 8]),
                                       in1=invf[:].unsqueeze(1).to_broadcast([128, NT, 8]), op=ALU.mult),
             reads=[b_trig, b_const], writes=[b_s1])
        dve1(lambda h: h.tensor_scalar(out=small2[:], in0=small[:], scalar1=float(1.0 / twopi), scalar2=None, op0=ALU.mult),
             reads=[b_s1], writes=[b_s2])
        dve1(lambda h: h.tensor_scalar(out=small2[:], in0=small2[:], scalar1=MAGIC, scalar2=None, op0=ALU.add),
             reads=[b_s2], writes=[b_s2])
        dve1(lambda h: h.tensor_scalar(out=small2[:], in0=small2[:], scalar1=-MAGIC, scalar2=None, op0=ALU.add),
             reads=[b_s2], writes=[b_s2])
        for c in (c1, c2, c3):
            dve1(lambda h, c=c: h.scalar_tensor_tensor(out=small[:], in0=small2[:], scalar=-c, in1=small[:],
                                                       op0=ALU.mult, op1=ALU.add),
                 reads=[b_s1, b_s2], writes=[b_s1])
        dve1(lambda h: h.tensor_scalar(out=small[:], in0=small[:], scalar1=3.1415925, scalar2=-3.1415925,
                                       op0=ALU.min, op1=ALU.max), reads=[b_s1], writes=[b_s1])
        dve1(lambda h: h.tensor_scalar(out=small2[:], in0=small[:], scalar1=-1.0, scalar2=None, op0=ALU.mult),
             reads=[b_s1, b_s2], writes=[b_s2])
        dve1(lambda h: h.tensor_tensor(out=small2[:], in0=small[:], in1=small2[:], op=ALU.max),
             reads=[b_s1, b_s2], writes=[b_s2])
        b_trig2 = Buf("trig2")
        P.op("act", lambda h: h.activation(out=sinT[:], in_=small[:], func=AF.Sin), reads=[b_s1], writes=[b_trig2])
        P.op("act", lambda h: h.activation(out=cosT[:], in_=small2[:], func=AF.Sin, scale=-1.0, bias=halfpi[:]),
             reads=[b_s2, b_const], writes=[b_trig2])

        def ring_next():
            i = ring_pos[0] % RING
            ring_pos[0] += 1
            return i

        def load_w(slot, half, dst_ap, src_ap, split=False):
            P.op("pool", lambda h: h.dma_start(out=dst_ap, in_=src_ap),
                 writes=([ringb[slot][half]] if split else ringb[slot]), dma=rings[slot][half])

        def transposes_to(src_tile_ap_fn, n, bankid, reads):
            for c in range(n):
                P.op("pe", lambda h, c=c: h.transpose(out=bankbf(bankid)[:, c * 128:(c + 1) * 128],
                                                       in_=src_tile_ap_fn(c), identity=ident[:]),
                     reads=reads + [b_const], writes=[PB[bankid]])

        def layer_norm_inplace(i, gbuf, bbuf, g_ap, b_ap):
            xi = X[:, i, :]

            P.op("dve", lambda h: h.bn_stats(out=lnst[:, 0, :], in_=X[:, i, 0:512]), reads=[Xb[i]], writes=[b_lnst])
            P.op("dve", lambda h: h.bn_stats(out=lnst[:, 1, :], in_=X[:, i, 512:1024]), reads=[Xb[i]], writes=[b_lnst2])
            P.op("dve", lambda h: h.bn_aggr(out=lnmv[:, 0:2], in_=lnst[:].rearrange("p a b -> p (a b)")),
                 reads=[b_lnst, b_lnst2], writes=[b_lnmv])
            P.op("dve", lambda h: h.tensor_scalar(out=lnmv[:, 2:3], in0=lnmv[:, 1:2], scalar1=EPS, scalar2=None, op0=ALU.add),
                 reads=[b_lnmv], writes=[b_lnmv])
            P.op("act", lambda h: h.activation(out=lnmv[:, 2:3], in_=lnmv[:, 2:3], func=AF.Sqrt),
                 reads=[b_lnmv], writes=[b_lnmv])
            P.op("dve", lambda h: h.reciprocal(out=lnmv[:, 2:3], in_=lnmv[:, 2:3]), reads=[b_lnmv], writes=[b_lnmv])
            P.op("dve", lambda h: h.tensor_scalar(out=lnmv[:, 3:4], in0=lnmv[:, 0:1], scalar1=lnmv[:, 2:3], scalar2=-1.0,
                                                  op0=ALU.mult, op1=ALU.mult), reads=[b_lnmv], writes=[b_lnmv])
            P.op("act", lambda h: h.activation(out=xi, in_=xi, func=AF.Identity, scale=lnmv[:, 2:3], bias=lnmv[:, 3:4]),
                 reads=[b_lnmv, Xb[i]], writes=[Xb[i]])
            P.op("dve", lambda h: h.tensor_tensor(out=xi, in0=xi, in1=g_ap, op=ALU.mult),
                 reads=[Xb[i], gbuf], writes=[Xb[i]])
            P.op("dve", lambda h: h.tensor_tensor(out=xi, in0=xi, in1=b_ap, op=ALU.add),
                 reads=[Xb[i], bbuf], writes=[Xb[i]])

        def load_vec(dst, buf, sem, name, l):
            P.op("sp", lambda h: h.dma_start(out=dst[:], in_=vec_d[name][l, :].partition_broadcast(128)),
                 writes=[buf], dma=sem)

        def body_layers():
          for l in range(L):
              stage(1)
              load_vec(sgg, b_sgg, s_sgg, "sgu_ln_g", l)
              load_vec(sgb, b_sgb, s_sgb, "sgu_ln_b", l)
              sl = ring_next()
              load_w(sl, 0, ring[sl][:, 0:1024].rearrange("p (g s) -> p g s", g=8),
                     sgu_w_d[l].rearrange("g t s -> t g s"))
              for g8 in range(8):
                  P.op("pe", lambda h, g8=g8, sl=sl: h.transpose(out=bankbf(0)[:, g8 * 128:(g8 + 1) * 128],
                                                                 in_=ring[sl][:, g8 * 128:(g8 + 1) * 128], identity=ident[:]),
                       reads=ringb[sl] + [b_const], writes=[PB[0]])
              P.op("dve", lambda h: h.tensor_tensor(out=WsT[:], in0=bankbf(0).rearrange("p (g t) -> p g t", g=8),
                                                    in1=tri[:].unsqueeze(1).to_broadcast([128, 8, 128]), op=ALU.mult),
                   reads=[PB[0], b_const], writes=[b_WsT])
              P.op("pool", lambda h, l=l: h.dma_start(out=sgub[:], in_=sgu_b_d[l]), writes=[b_sgub], dma=s_sgub)

              for g in range(NG):
                  stage(2 + g * 10)
                  P.op("dve", lambda h: h.memset(mx8[:, 0:1], 0.0), writes=[TOK])
                  for tt in range(4):
                      i = 4 * g + tt
                      P.op("act", lambda h, i=i: h.activation(func=AF.Identity, out=xb[:], in_=X[:, i, :]), reads=[Xb[i]], writes=[b_xb])
                      tb = tt % 2
                      transposes_to(lambda c: xb[:, c * 128:(c + 1) * 128], 8, tb, [b_xb])
                      P.op("dve", lambda h, tt=tt, tb=tb: h.tensor_copy(
                          out=xT[:, :, tt * 128:(tt + 1) * 128], in_=bankbf(tb).rearrange("p (c t) -> p c t", c=8)),
                          reads=[PB[tb]], writes=[b_xT[tt]])
                  stage(2.1 + g * 10)
                  wsl = []
                  for ci in range(3):
                      c0, cw = CH[ci]
                      sl = ring_next()
                      wsl.append(sl)
                      load_w(sl, 0, ring[sl][:, 0:8 * cw].rearrange("p (c n) -> p c n", c=8),
                             w_in_d[l, :, c0:c0 + cw].rearrange("(c p) n -> p c n", p=128))
                  for tt in range(4):
                      i = 4 * g + tt
                      for ci in range(3):
                          c0, cw = CH[ci]
                          sl = wsl[ci]
                          bk = 2 + ci
                          wv = ring[sl][:, 0:8 * cw].rearrange("p (c n) -> p c n", c=8)
                          for kc in range(8):
                              P.op("pe", lambda h, kc=kc, bk=bk, wv=wv, cw=cw, tt=tt: h.matmul(
                                  bank(bk)[:, 0:cw], lhsT=xT[:, kc, tt * 128:(tt + 1) * 128], rhs=wv[:, kc, :],
                                  start=(kc == 0), stop=(kc == 7)),
                                  reads=[b_xT[tt]] + ringb[sl], writes=[PB[bk]])
                      stage(2.2 + g * 10)
                      bA = bank(2)
                      x1 = bA[:, 0:120].rearrange("p (h d) -> p h d", d=8)
                      x2 = bA[:, 120:240].rearrange("p (h d) -> p h d", d=8)
                      cb = cosT[:, i:i + 1, :].to_broadcast([128, 15, 8])
                      sbb = sinT[:, i:i + 1, :].to_broadcast([128, 15, 8])
                      t1 = rt[0][:, 0:120].rearrange("p (h d) -> p h d", d=8)
                      t2 = rt[1][:, 0:120].rearrange("p (h d) -> p h d", d=8)
                      t3 = rt[2][:, 0:120].rearrange("p (h d) -> p h d", d=8)
                      t4 = rt[3][:, 0:120].rearrange("p (h d) -> p h d", d=8)

                      def rope(h, x1=x1, x2=x2, cb=cb, sbb=sbb, tt=tt):
                          h.tensor_tensor(out=t1, in0=x1, in1=cb, op=ALU.mult)
                          h.tensor_tensor(out=t2, in0=x2, in1=sbb, op=ALU.mult)
                          h.tensor_tensor(out=t3, in0=x2, in1=cb, op=ALU.mult)
                          return h.tensor_tensor(out=t4, in0=x1, in1=sbb, op=ALU.mult)
                      P.op("dve", rope, reads=[PB[2], b_trig2], writes=b_rt)

                      def rope2(h, tt=tt):
                          h.tensor_tensor(out=QK[:, tt, 0:15, 0:8], in0=t1, in1=t2, op=ALU.subtract)
                          return h.tensor_tensor(out=QK[:, tt, 0:15, 8:16], in0=t3, in1=t4, op=ALU.add)
                      P.op("dve", rope2, reads=b_rt, writes=[b_QK[tt]])

                      def rope3(h, tt=tt):
                          h.tensor_tensor(out=QK[:, tt, 15, 0:8], in0=rt[0][:, 112:120], in1=rt[1][:, 112:120], op=ALU.subtract)
                          return h.tensor_tensor(out=QK[:, tt, 15, 8:16], in0=rt[2][:, 112:120], in1=rt[3][:, 112:120],
                                                 op=ALU.add)
                      if "b" in KDBG:
                          P.op("dve", rope3, reads=b_rt, writes=[b_QK[tt]])
                      if "c" in KDBG:
                          P.op("act", lambda h, tt=tt, bA=bA: h.activation(func=AF.Identity, out=QK[:, tt, 15, 16:64],
                                                                           in_=bA[:, 436:484]),
                               reads=[PB[2]], writes=[b_QK[tt]])
                      stage(2.3 + g * 10)
                      P.op("dve", lambda h, tt=tt, bA=bA: h.tensor_scalar(out=wi[:, tt, :], in0=bA[:, 240:244], scalar1=1.0 / 16,
                                                                          scalar2=None, op0=ALU.mult),
                           reads=[PB[2]], writes=[b_wi[tt]])
                      stage(2.3 + 0.01 * 1 + g * 10)
                      P.op("act", lambda h, tt=tt, bA=bA: h.activation(func=AF.Identity, out=QK[:, tt, 10:15, 16:64],
                                                                 in_=bA[:, 244:484].rearrange("p (h d) -> p h d", d=48)),
                           reads=[PB[2]], writes=[b_QK[tt]])
                      stage(2.3 + 0.01 * 2 + g * 10)
                      P.op("act", lambda h, tt=tt: h.activation(func=AF.Identity, out=QK[:, tt, 0:10, 16:64],
                                                          in_=bank(3)[:, 0:480].rearrange("p (h d) -> p h d", d=48)),
                           reads=[PB[3]], writes=[b_QK[tt]])
                      stage(2.3 + 0.01 * 3 + g * 10)
                      P.op("act", lambda h, i=i: h.activation(func=AF.Identity, out=V[:, i, :, 0:64],
                                                        in_=bank(4)[:, 0:128].rearrange("p (g d) -> p g d", d=64)),
                           reads=[PB[4]], writes=[b_V[i]])
                      stage(2.3 + 0.01 * 4 + g * 10)
                      stage(2.4 + g * 10)
                      tb = tt % 2
                      qkf = QK[:, tt, :, :].rearrange("p s d -> p (s d)")
                      transposes_to(lambda c, qkf=qkf: qkf[:, c * 128:(c + 1) * 128], 8, tb, [b_QK[tt]])
                      bb = bankbf(tb).rearrange("p (c t) -> p c t", c=8)
                      P.op("dve", lambda h, tt=tt, bb=bb: h.tensor_copy(out=qT[:, :, tt * 128:(tt + 1) * 128], in_=bb[:, 0:4, :]),
                           reads=[PB[tb]], writes=[b_qT[tt]])
                      P.op("dve", lambda h, i=i, bb=bb: h.tensor_copy(out=kT[:, i * 128:(i + 1) * 128], in_=bb[:, 4, :]),
                           reads=[PB[tb]], writes=[b_kT[i]])
                      P.op("dve", lambda h, tt=tt, bb=bb: h.tensor_copy(out=qiT[:, :, tt * 128:(tt + 1) * 128], in_=bb[:, 5:7, :]),
                           reads=[PB[tb]], writes=[b_qiT[tt]])
                      P.op("dve", lambda h, i=i, bb=bb: h.tensor_copy(out=kiT[:, i * 128:(i + 1) * 128], in_=bb[:, 7, :]),
                           reads=[PB[tb]], writes=[b_kiT[i]])

                  stage(3 + g * 10)
                  sD, sE = ring_next(), ring_next()
                  for sl, ci in ((sD, 3), (sE, 4)):
                      c0, cw = CH[ci]
                      load_w(sl, 0, ring[sl][:, 0:8 * cw].rearrange("p (c n) -> p c n", c=8),
                             w_in_d[l, :, c0:c0 + cw].rearrange("(c p) n -> p c n", p=128))
                  for tt in range(4):
                      i = 4 * g + tt
                      for sl, bk in ((sD, 5), (sE, 6)):
                          wv = ring[sl][:].rearrange("p (c n) -> p c n", c=8)
                          for kc in range(8):
                              P.op("pe", lambda h, kc=kc, bk=bk, wv=wv, tt=tt: h.matmul(
                                  bank(bk), lhsT=xT[:, kc, tt * 128:(tt + 1) * 128], rhs=wv[:, kc, :],
                                  start=(kc == 0), stop=(kc == 7)),
                                  reads=[b_xT[tt]] + ringb[sl], writes=[PB[bk]])
                      P.op("act", lambda h: h.activation(out=gu[:], in_=bank(5), func=AF.Gelu_apprx_tanh),
                           reads=[PB[5]], writes=[b_gu])
                      P.op("act", lambda h: h.activation(out=gv[:], in_=bank(6), func=AF.Gelu_apprx_tanh),
                           reads=[PB[6]], writes=[b_gv])

                      P.op("dve", lambda h: h.bn_stats(out=lnst[:, 0, :], in_=gv[:]), reads=[b_gv], writes=[b_lnst])
                      P.op("dve", lambda h: h.bn_aggr(out=lnmv[:, 0:2], in_=lnst[:, 0, :]), reads=[b_lnst], writes=[b_lnmv])
                      P.op("dve", lambda h: h.tensor_scalar(out=lnmv[:, 2:3], in0=lnmv[:, 1:2], scalar1=EPS, scalar2=None,
                                                            op0=ALU.add), reads=[b_lnmv], writes=[b_lnmv])
                      P.op("act", lambda h: h.activation(out=lnmv[:, 2:3], in_=lnmv[:, 2:3], func=AF.Sqrt),
                           reads=[b_lnmv], writes=[b_lnmv])
                      P.op("dve", lambda h: h.reciprocal(out=lnmv[:, 2:3], in_=lnmv[:, 2:3]), reads=[b_lnmv], writes=[b_lnmv])
                      P.op("dve", lambda h: h.tensor_scalar(out=lnmv[:, 3:4], in0=lnmv[:, 0:1], scalar1=lnmv[:, 2:3],
                                                            scalar2=-1.0, op0=ALU.mult, op1=ALU.mult),
                           reads=[b_lnmv], writes=[b_lnmv])
                      P.op("act", lambda h: h.activation(out=gv[:], in_=gv[:], func=AF.Identity, scale=lnmv[:, 2:3],
                                                         bias=lnmv[:, 3:4]),
                           reads=[b_lnmv, b_gv], writes=[b_gv])
                      P.op("dve", lambda h: h.tensor_tensor(out=gv[:], in0=gv[:], in1=sgg[:], op=ALU.mult),
                           reads=[b_gv, b_sgg], writes=[b_gv])
                      P.op("dve", lambda h: h.tensor_tensor(out=gvn[:], in0=gv[:], in1=sgb[:], op=ALU.add),
                           reads=[b_gv, b_sgb], writes=[b_gvn])
                      for g8 in range(8):
                          P.op("pe", lambda h, g8=g8: h.matmul(bank(7)[:, g8 * 64:(g8 + 1) * 64], lhsT=WsT[:, g8, :],
                                                               rhs=gvn[:, g8 * 64:(g8 + 1) * 64],
                                                               start=(g8 == 0), stop=False, skip_group_check=True),
                               reads=[b_WsT, b_gvn], writes=[PB[7]])
                      P.op("pe", lambda h: h.matmul(bank(7), lhsT=sgub[:], rhs=E8[:], start=False, stop=True,
                                                    skip_group_check=True),
                           reads=[b_sgub, b_const], writes=[PB[7]])
                      P.op("dve", lambda h, tt=tt: h.tensor_tensor(out=bout[:, tt, :], in0=gu[:], in1=bank(7), op=ALU.mult),
                           reads=[b_gu, PB[7]], writes=[b_bout[tt]])

                  stage(4 + g * 10)
                  sO = [ring_next(), ring_next()]
                  for hh in range(2):
                      load_w(sO[hh], 0, ring[sO[hh]][:].rearrange("p (c n) -> p c n", c=8),
                             w_o_d[l, :, hh * 512:(hh + 1) * 512].rearrange("(c p) n -> p c n", p=128))
                  load_vec(lng, b_lng, s_lng, "ln1_g", l)
                  load_vec(lnb, b_lnb, s_lnb, "ln1_b", l)

                  def prep(tt):
                      i = 4 * g + tt
                      N = 128 * (i + 1)
                      mk = mask[i % 2]
                      bmk = b_mask[i % 2]
                      sc = score2[i % 2]
                      bsc = b_score2[i % 2]
                      thunks = []
                      if i < 2:
                          P.op("dve", lambda h: h.memset(mk[:, 0:N], 1.0), writes=[bmk])
                          P.op("dve", lambda h: h.memset(mk[0:64, N - 64:N], 0.0), reads=[bmk], writes=[bmk])
                          return thunks

                      def dg(h):
                          for hh in range(4):
                              r = h.tensor_scalar(out=diagw[:, hh, :], in0=ident[:], scalar1=wi[:, tt, hh:hh + 1],
                                                  scalar2=None, op0=ALU.mult)
                          return r
                      P.op("dve", dg, reads=[b_wi[tt], b_const], writes=[b_diagw])
                      ncs = (N + 511) // 512
                      for c in range(ncs):
                          wc = min(512, N - 512 * c)
                          rb = c % 2
                          for hh in range(4):
                              pr = (hh % 2) * 64
                              P.op("pe", lambda h, hh=hh, pr=pr, c=c, wc=wc: h.matmul(
                                  bank(hh)[:, 0:wc], lhsT=qiT[pr:pr + 64, hh // 2, tt * 128:(tt + 1) * 128],
                                  rhs=kiT[pr:pr + 64, c * 512:c * 512 + wc], start=True, stop=True),
                                  reads=[b_qiT[tt]] + b_kiT[4 * c:4 * c + 4], writes=[PB[hh]])
                          P.op("act", lambda h, rb=rb, wc=wc: h.activation(out=relu[rb][:, :, 0:wc], in_=PS[:, 0:4, 0:wc],
                                                                           func=AF.Relu),
                               reads=PB[0:4], writes=[b_relu[rb]])
                          sbk = 4 + (c % 2)
                          for hh in range(4):
                              P.op("pe", lambda h, hh=hh, rb=rb, wc=wc, sbk=sbk: h.matmul(
                                  bank(sbk)[:, 0:wc], lhsT=diagw[:, hh, :], rhs=relu[rb][:, hh, 0:wc],
                                  start=(hh == 0), stop=(hh == 3)),
                                  reads=[b_diagw, b_relu[rb]], writes=[PB[sbk]])
                          P.op("dve", lambda h, c=c, wc=wc, sbk=sbk: h.tensor_copy(out=sc[:, c * 512:c * 512 + wc],
                                                                                   in_=bank(sbk)[:, 0:wc]),
                               reads=[PB[sbk]], writes=[bsc])

                      def B1(fn, reads=(), writes=()):
                          P.op("dve", fn, reads=list(reads), writes=list(writes))

                      K_ = NBIS

                      def t_init():
                          B1(lambda h: h.tensor_reduce(out=bis[:, 0:1], in_=sc[:, 0:N], axis=AX.X, op=ALU.min),
                             reads=[bsc], writes=[b_bis])
                          B1(lambda h: h.tensor_reduce(out=bis[:, 5:6], in_=sc[:, 0:N], axis=AX.X, op=ALU.max),
                             reads=[bsc], writes=[b_bis2])
                          B1(lambda h: h.memset(sc[0:64, N - 64:N], -BIG), reads=[b_bis, b_bis2], writes=[bsc])
                          B1(lambda h: h.tensor_tensor(out=bis[:, 1:2], in0=bis[:, 5:6], in1=bis[:, 0:1], op=ALU.subtract),
                             reads=[b_bis, b_bis2], writes=[b_bis2])
                          B1(lambda h: h.tensor_scalar(out=bis[:, 1:2], in0=bis[:, 1:2], scalar1=1.0001, scalar2=1e-6,
                                                       op0=ALU.mult, op1=ALU.add), reads=[b_bis2], writes=[b_bis2])
                          B1(lambda h: h.tensor_scalar(out=b2[:, 32:33 + K_], in0=b2[:, 0:1 + K_], scalar1=bis[:, 1:2],
                                                       scalar2=None, op0=ALU.mult),
                             reads=[b_bis2, b_const], writes=[b_NW])
                          B1(lambda h: h.scalar_tensor_tensor(out=b2[:, 64:65], in0=bis[:, 0:1], scalar=-1.0,
                                                              in1=b2[:, 32:33], op0=ALU.mult, op1=ALU.add),
                             reads=[b_bis, b_NW], writes=[b_nm[0]])
                          B1(lambda h: h.memset(b2[:, 68:69], float(N - 511)), writes=[b_cN])
                      thunks.append(t_init)
                      for k in range(1, K_ + 1):
                          def t_it(k=k):
                              cur, nx = (k - 1) % 2, k % 2
                              P.op("act", lambda h: h.activation(out=mk[:, 0:N], in_=sc[:, 0:N], func=AF.Sign,
                                                                 bias=b2[:, 64 + cur:65 + cur], scale=1.0,
                                                                 accum_out=b2[:, 66:67]),
                                   reads=[bsc, b_nm[cur]], writes=[b_ss, bmk])
                              P.op("act", lambda h: h.activation(out=b2[:, 67:68], in_=b2[:, 66:67], func=AF.Sign,
                                                                 bias=b2[:, 68:69], scale=1.0),
                                   reads=[b_ss, b_cN], writes=[b_sg])
                              if k < K_:
                                  P.op("act", lambda h: h.activation(out=b2[:, 64 + nx:65 + nx], in_=b2[:, 67:68],
                                                                     func=AF.Identity, scale=b2[:, 32 + k:33 + k],
                                                                     bias=b2[:, 64 + cur:65 + cur]),
                                       reads=[b_sg, b_NW, b_nm[cur]], writes=[b_nm[nx]])
                          thunks.append(t_it)

                      def t_fin():
                          cK = (K_ - 1) % 2
                          B1(lambda h: h.tensor_scalar(out=b2[:, 69:70], in0=b2[:, 67:68], scalar1=-1.0, scalar2=0.5,
                                                       op0=ALU.add, op1=ALU.mult), reads=[b_sg], writes=[b_a])
                          B1(lambda h: h.scalar_tensor_tensor(out=b2[:, 70:71], in0=b2[:, 69:70],
                                                              scalar=b2[:, 32 + K_ - 1:32 + K_], in1=b2[:, 64 + cK:65 + cK],
                                                              op0=ALU.mult, op1=ALU.add),
                             reads=[b_a, b_NW, b_nm[cK]], writes=[b_nthr])
                          B1(lambda h: h.tensor_scalar(out=mk[:, 0:N], in0=sc[:, 0:N], scalar1=b2[:, 70:71], scalar2=0.0,
                                                       op0=ALU.add, op1=ALU.is_ge),
                             reads=[bsc, b_nthr], writes=[bmk])
                      thunks.append(t_fin)
                      return thunks

                  def attn(tt, sO=sO):
                      i = 4 * g + tt
                      mk = mask[i % 2]
                      bmk = b_mask[i % 2]
                      thunks = []
                      for j in range(i + 1):
                          def t_j(j=j):
                              pb0 = 0 if j % 2 == 0 else 2
                              eb = j % 2
                              for jj in range(4):
                                  for grp in range(2):
                                      pr = grp * 64
                                      P.op("pe", lambda h, jj=jj, grp=grp, pr=pr: h.matmul(
                                          bank(pb0 + grp)[:, jj * 128:(jj + 1) * 128],
                                          lhsT=kT[pr:pr + 64, j * 128:(j + 1) * 128],
                                          rhs=qT[pr:pr + 64, jj, tt * 128:(tt + 1) * 128], start=True, stop=True),
                                          reads=[b_kT[j], b_qT[tt]], writes=[PB[pb0 + grp]])
                              P.op("act", lambda h: h.activation(
                                  out=expP[eb][:].rearrange("p (b n) -> p b n", b=2), in_=PS[:, pb0:pb0 + 2, :],
                                  func=AF.Exp, scale=0.125),
                                  reads=[PB[pb0], PB[pb0 + 1]], writes=[b_expP[eb]])
                              P.op("pe", lambda h: h.transpose(out=bankbf(6)[:, 0:128], in_=mk[:, j * 128:(j + 1) * 128],
                                                               identity=ident[:]),
                                   reads=[bmk, b_const], writes=[PB[6]])
                              P.op("dve", lambda h: h.tensor_tensor(
                                  out=Pm[eb][:], in0=expP[eb][:].rearrange("p (a t) -> p a t", a=8),
                                  in1=bankbf(6)[:, 0:128].unsqueeze(1).to_broadcast([128, 8, 128]), op=ALU.mult),
                                  reads=[b_expP[eb], PB[6]], writes=[b_Pm[eb]])
                              for grp in range(2):
                                  for jj in range(4):
                                      P.op("pe", lambda h, grp=grp, jj=jj: h.matmul(
                                          bank(4 + grp)[:, jj * 65:(jj + 1) * 65], lhsT=Pm[eb][:, grp * 4 + jj, :],
                                          rhs=V[:, j, grp, :], start=(j == 0 and jj == 0), stop=(j == i and jj == 3),
                                          skip_group_check=True),
                                          reads=[b_Pm[eb], b_V[j]], writes=[PB[4 + grp]])
                          thunks.append(t_j)

                      def t_final():
                          pv = PS[:, 4:6, 0:260].rearrange("p b (h e) -> p b h e", e=65)
                          P.op("dve", lambda h: h.reciprocal(out=rs[:].rearrange("p (b h) -> p b h", b=2), in_=pv[:, :, :, 64]),
                               reads=[PB[4], PB[5]], writes=[b_rs])
                          for grp in range(2):
                              P.op("dve", lambda h, grp=grp: h.tensor_tensor(
                                  out=cat[:, grp * 256:(grp + 1) * 256].rearrange("p (h d) -> p h d", d=64),
                                  in0=pv[:, grp, :, 0:64],
                                  in1=rs[:, grp * 4:(grp + 1) * 4].unsqueeze(2).to_broadcast([128, 4, 64]), op=ALU.mult),
                                  reads=[PB[4 + grp], b_rs], writes=[b_cat])
                          P.op("act", lambda h: h.activation(func=AF.Identity, out=cat[:, 512:1024], in_=bout[:, tt, :]),
                               reads=[b_bout[tt]], writes=[b_cat])
                          transposes_to(lambda c: cat[:, c * 128:(c + 1) * 128], 8, 7, [b_cat])
                          P.op("dve", lambda h: h.tensor_copy(out=catT[:], in_=bankbf(7).rearrange("p (c t) -> p c t", c=8)),
                               reads=[PB[7]], writes=[b_catT])
                          for hh in range(2):
                              wv = ring[sO[hh]][:].rearrange("p (c n) -> p c n", c=8)
                              for kc in range(8):
                                  P.op("pe", lambda h, hh=hh, kc=kc, wv=wv: h.matmul(
                                      bank(hh), lhsT=catT[:, kc, :], rhs=wv[:, kc, :], start=(kc == 0), stop=(kc == 7)),
                                      reads=[b_catT] + ringb[sO[hh]], writes=[PB[hh]])
                          P.op("dve", lambda h: h.scalar_tensor_tensor(
                              out=X[:, i, :].rearrange("p (b n) -> p b n", b=2), in0=X[:, i, :].rearrange("p (b n) -> p b n", b=2),
                              scalar=ALPHA, in1=PS[:, 0:2, :], op0=ALU.mult, op1=ALU.add),
                              reads=[Xb[i], PB[0], PB[1]], writes=[Xb[i]])
                          layer_norm_inplace(i, b_lng, b_lnb, lng[:], lnb[:])
                      return thunks, t_final

                  for t_ in prep(0):
                      t_()
                  for tt in range(4):
                      nxt = prep(tt + 1) if tt < 3 else []
                      A_, fin_ = attn(tt)
                      na, nb, bi = len(A_), len(nxt), 0
                      for k_, a_ in enumerate(A_):
                          a_()
                          tgt = (nb * (k_ + 1)) // na
                          while bi < tgt:
                              nxt[bi]()
                              bi += 1
                      fin_()
                      while bi < nb:
                          nxt[bi]()
                          bi += 1

                  stage(6 + g * 10)
                  P.op("dve", lambda h: h.memset(mx8[:, 0:1], 0.0), writes=[TOK])
                  for tt in range(4):
                      i = 4 * g + tt
                      P.op("act", lambda h, i=i: h.activation(func=AF.Identity, out=xb[:], in_=X[:, i, :]), reads=[Xb[i]], writes=[b_xb])
                      tb = tt % 2
                      transposes_to(lambda c: xb[:, c * 128:(c + 1) * 128], 8, tb, [b_xb])
                      P.op("dve", lambda h, tt=tt, tb=tb: h.tensor_copy(
                          out=xT[:, :, tt * 128:(tt + 1) * 128], in_=bankbf(tb).rearrange("p (c t) -> p c t", c=8)),
                          reads=[PB[tb]], writes=[b_xT[tt]])
                  P.op("pool", lambda h, l=l, g=g: h.dma_start(
                      out=p_sb[:], in_=p_d[l, 512 * g:512 * g + 512, :].rearrange("(n p) d -> p n d", p=128)),
                      writes=[b_psb], dma=s_psb)
                  for tt in range(4):
                      for c in range(2):
                          P.op("pe", lambda h, tt=tt, c=c: h.transpose(
                              out=bankbf(2)[:, (tt * 2 + c) * 128:(tt * 2 + c + 1) * 128],
                              in_=p_sb[:, tt, c * 128:(c + 1) * 128], identity=ident[:]),
                              reads=[b_psb, b_const], writes=[PB[2]])
                  P.op("dve", lambda h: h.tensor_copy(out=pT[:].rearrange("p c (t n) -> p t c n", t=4),
                                               in_=bankbf(2).rearrange("p (t c n) -> p t c n", t=4, c=2)),
                       reads=[PB[2]], writes=[b_pT])
                  for fc in range(NFC):
                      sl = ring_next()
                      wv = ring[sl][:, 0:2048].rearrange("p (c n) -> p c n", c=8)
                      load_w(sl, 0, wv[:, :, 0:128], w_fi_d[l, :, fc * 128:(fc + 1) * 128].rearrange("(c p) n -> p c n", p=128),
                             split=True)
                      load_w(sl, 1, wv[:, :, 128:256],
                             w_fi_d[l, :, DFF + fc * 128:DFF + (fc + 1) * 128].rearrange("(c p) n -> p c n", p=128), split=True)
                      gb = 3 + 2 * (fc % 2)
                      for hf in range(2):
                          for kc in range(8):
                              P.op("pe", lambda h, hf=hf, kc=kc, wv=wv, gb=gb: h.matmul(
                                  bank(gb + hf), lhsT=wv[:, kc, hf * 128:(hf + 1) * 128], rhs=xT[:, kc, :],
                                  start=(kc == 0), stop=(kc == 7)),
                                  reads=b_xT + [ringb[sl][hf]], writes=[PB[gb + hf]])
                      sb_ = fc % 2
                      P.op("act", lambda h, gb=gb, sb_=sb_: h.activation(out=silu[sb_][:], in_=bank(gb), func=AF.Silu),
                           reads=[PB[gb]], writes=[b_silu[sb_]])
                      P.op("dve", lambda h, gb=gb, sb_=sb_, fc=fc: h.tensor_tensor(out=gT[:, fc, :], in0=silu[sb_][:],
                                                                                   in1=bank(gb + 1), op=ALU.mult),
                           reads=[b_silu[sb_], PB[gb + 1]], writes=[b_gT[fc]])
                  for n8 in range(8):
                      sl = ring_next()
                      wv = ring[sl][:, 0:NFC * 128].rearrange("p (c n) -> p c n", c=NFC)
                      load_w(sl, 0, wv, w_fo_d[l, :, n8 * 128:(n8 + 1) * 128].rearrange("(c p) n -> p c n", p=128))
                      ob = n8 % 2
                      for fc in range(NFC):
                          P.op("pe", lambda h, fc=fc, wv=wv, ob=ob: h.matmul(
                              bank(ob), lhsT=wv[:, fc, :], rhs=gT[:, fc, :], start=(fc == 0), stop=(fc == NFC - 1)),
                              reads=[b_gT[fc]] + ringb[sl], writes=[PB[ob]])
                      P.op("act", lambda h, ob=ob: h.activation(func=AF.Identity, out=ffnT[ob][:], in_=bank(ob)), reads=[PB[ob]], writes=[b_ffnT[ob]])
                      for tt in range(4):
                          P.op("pe", lambda h, tt=tt, ob=ob: h.transpose(out=bank(2)[:, tt * 128:(tt + 1) * 128],
                                                                         in_=ffnT[ob][:, tt * 128:(tt + 1) * 128],
                                                                         identity=identf[:]),
                               reads=[b_ffnT[ob], b_const], writes=[PB[2]])
                      P.op("dve", lambda h, n8=n8, g=g: h.scalar_tensor_tensor(
                          out=X[:, 4 * g:4 * g + 4, n8 * 128:(n8 + 1) * 128], in0=X[:, 4 * g:4 * g + 4, n8 * 128:(n8 + 1) * 128],
                          scalar=ALPHA, in1=bank(2).rearrange("p (t n) -> p t n", t=4), op0=ALU.mult, op1=ALU.add),
                          reads=[PB[2]] + Xb[4 * g:4 * g + 4], writes=Xb[4 * g:4 * g + 4])
                  stage(7 + g * 10)
                  sP = ring_next()
                  wpl = ring[sP][:, 0:2048].rearrange("p (c n) -> p c n", c=2)
                  load_w(sP, 0, wpl, w_ple_d[l].rearrange("(c p) n -> p c n", p=128))
                  sG = [ring_next(), ring_next()]
                  for hh in range(2):
                      load_w(sG[hh], 0, ring[sG[hh]][:].rearrange("p (c n) -> p c n", c=8),
                             w_pg_d[l, :, hh * 512:(hh + 1) * 512].rearrange("(c p) n -> p c n", p=128))
                  load_vec(lng, b_lng, s_lng, "ln2_g", l)
                  load_vec(lnb, b_lnb, s_lnb, "ln2_b", l)
                  for tt in range(4):
                      i = 4 * g + tt
                      for hh in range(2):
                          for c in range(2):
                              P.op("pe", lambda h, hh=hh, c=c, tt=tt, wpl=wpl: h.matmul(
                                  bank(4 + hh), lhsT=pT[:, c, tt * 128:(tt + 1) * 128], rhs=wpl[:, c, hh * 512:(hh + 1) * 512],
                                  start=(c == 0), stop=(c == 1)),
                                  reads=[b_pT] + ringb[sP], writes=[PB[4 + hh]])
                          wv = ring[sG[hh]][:].rearrange("p (c n) -> p c n", c=8)
                          for kc in range(8):
                              P.op("pe", lambda h, hh=hh, kc=kc, wv=wv, tt=tt: h.matmul(
                                  bank(6 + hh), lhsT=xT[:, kc, tt * 128:(tt + 1) * 128], rhs=wv[:, kc, :],
                                  start=(kc == 0), stop=(kc == 7)),
                                  reads=[b_xT[tt]] + ringb[sG[hh]], writes=[PB[6 + hh]])
                      P.op("act", lambda h: h.activation(out=sig[:].rearrange("p (b n) -> p b n", b=2), in_=PS[:, 6:8, :],
                                                         func=AF.Sigmoid),
                           reads=[PB[6], PB[7]], writes=[b_sig])
                      P.op("dve", lambda h: h.tensor_tensor(out=sig[:].rearrange("p (b n) -> p b n", b=2),
                                                            in0=sig[:].rearrange("p (b n) -> p b n", b=2),
                                                            in1=PS[:, 4:6, :], op=ALU.mult),
                           reads=[b_sig, PB[4], PB[5]], writes=[b_sig])
                      P.op("dve", lambda h, i=i: h.tensor_tensor(out=X[:, i, :], in0=X[:, i, :], in1=sig[:], op=ALU.add),
                           reads=[b_sig, Xb[i]], writes=[Xb[i]])
                      layer_norm_inplace(i, b_lng, b_lnb, lng[:], lnb[:])

        try:
            body_layers()
        except _Stop:
            pass
        for g in range(NG):
            P.op("sp", lambda h, g=g: h.dma_start(
                out=out_d[512 * g:512 * g + 512, :].rearrange("(n p) d -> p n d", p=128),
                in_=X[:, 4 * g:4 * g + 4, :]),
                reads=Xb[4 * g:4 * g + 4], dma=s_out[g])
        fin = Buf("fin")
        for g in range(NG):
            pass
        last = [len(P.ops) - NG + g for g in range(NG)]
        idx = P.op("sp", lambda h: None)
        P.ops[idx][2].update(last)
        P.emit()
    return nc


def prep_inputs(inputs, n_layers, b):
    perm = w_in_perm()
    L = n_layers
    m = {}
    m["x"] = np.ascontiguousarray(inputs["x"][b], dtype=np.float32)
    m["p"] = np.ascontiguousarray(inputs["p"][:L, b], dtype=np.float32)
    m["pos"] = np.ascontiguousarray(np.asarray(inputs["positions"][b]).reshape(NT, 128).T.astype(np.int32))
    return m


def shared_inputs(inputs, n_layers, l0=0):
    perm = w_in_perm()
    L = n_layers
    sl = slice(l0, l0 + L)
    m = {}
    m["w_in"] = np.ascontiguousarray(np.asarray(inputs["w_in"])[sl][:, :, perm], dtype=np.float32)
    for k in ["w_o", "ln1_g", "ln1_b", "ln2_g", "ln2_b", "sgu_ln_g", "sgu_ln_b", "sgu_w", "sgu_b",
              "w_ffn_in", "w_ffn_out", "w_ple", "w_ple_gate"]:
        m[k] = np.ascontiguousarray(np.asarray(inputs[k])[sl], dtype=np.float32)
    return m


_NC_CACHE = {}


def kernel(**inputs):
    inputs = {k: np.asarray(v) for k, v in inputs.items()}
    L = 4
    if L not in _NC_CACHE:
        _NC_CACHE[L] = build(L)
    nc = _NC_CACHE[L]
    sh = shared_inputs(inputs, L)
    in_maps = []
    for b in range(8):
        m = prep_inputs(inputs, L, b)
        m.update(sh)
        in_maps.append(m)
    res = run_bass_kernel_spmd(nc, in_maps, core_ids=list(range(8)))
    out = np.stack([np.asarray(r["out"], dtype=np.float32) for r in res.results], axis=0)
    return out
```
